# Optimizing a Trainium2 kernel written in Bass

```python
import jax, jax.numpy as jnp
from jax import lax
import numpy as np

D_MODEL = 2048
BATCH = 2
SEQ = 4096
DEPTH = 4

N_BRANCH = 3
NORM_EPS = 1e-6
POOL_WINDOWS = (2, 4, 8, 16)
POOL_GROUPS = 4
POOL_WIDTH = D_MODEL // 2
POOL_GROUP_DIM = POOL_WIDTH // POOL_GROUPS
SSM_INNER = D_MODEL
SSM_HEAD_DIM = 64
SSM_HEADS = SSM_INNER // SSM_HEAD_DIM
SSM_GROUPS = 4
SSM_HEADS_PER_GROUP = SSM_HEADS // SSM_GROUPS
SSM_STATE = 128
SSM_CONV = 4
SSM_CONV_DIM = SSM_INNER + 2 * SSM_GROUPS * SSM_STATE
SSM_CHUNK = 128
SC_WIDTH = D_MODEL // 2
SC_KERNEL = 3
IN_SPLITS = (POOL_WIDTH, POOL_WIDTH, SSM_INNER, SSM_CONV_DIM, SSM_HEADS, SC_WIDTH, SC_WIDTH, SC_WIDTH, SC_WIDTH, N_BRANCH * D_MODEL)
IN_PROJ_DIM = 2 * POOL_WIDTH + SSM_INNER + SSM_CONV_DIM + SSM_HEADS + 4 * SC_WIDTH + N_BRANCH * D_MODEL

kernel_name = 'hybrid_pool_ssd_shortconv_gated_merge'


def rms_norm(x, w):
    xf = x.astype(jnp.float32)
    xf = xf * lax.rsqrt(jnp.mean(xf * xf, axis=-1, keepdims=True) + NORM_EPS)
    return (xf * w.astype(jnp.float32)).astype(x.dtype)


def causal_depthwise_conv(x, w):
    k, c = w.shape
    return lax.conv_general_dilated(x, w[:, None, :].astype(x.dtype), window_strides=(1,), padding=[(k - 1, 0)], dimension_numbers=('NWC', 'WIO', 'NWC'), feature_group_count=c)


def pool_mixer(u, gate, w_grp, scale):
    b, s, _ = u.shape
    cs = jnp.cumsum(u.astype(jnp.float32), axis=1)
    pos = jnp.arange(1, s + 1, dtype=jnp.float32)[None, :, None]
    outs = []
    for g, win in enumerate(POOL_WINDOWS):
        sl = slice(g * POOL_GROUP_DIM, (g + 1) * POOL_GROUP_DIM)
        csg = cs[..., sl]
        lagged = jnp.pad(csg[:, :s - win], ((0, 0), (win, 0), (0, 0)))
        mean = (csg - lagged) / jnp.minimum(pos, win)
        outs.append(mean - u[..., sl].astype(jnp.float32))
    d = jnp.stack(outs, axis=2).astype(u.dtype)
    y = jnp.einsum('bsgc,gcd->bsgd', d, w_grp).reshape(b, s, POOL_WIDTH)
    return y * scale * jax.nn.silu(gate)


def ssd_mixer(xbc, z, dt_raw, conv_w, conv_b, dt_bias, a_log, d_skip, norm_w):
    b, s, _ = xbc.shape
    c, l = s // SSM_CHUNK, SSM_CHUNK
    G, E, P, N = SSM_GROUPS, SSM_HEADS_PER_GROUP, SSM_HEAD_DIM, SSM_STATE
    f32 = jnp.float32
    xbc = jax.nn.silu(causal_depthwise_conv(xbc, conv_w) + conv_b)
    xs, bm, cm = jnp.split(xbc, [SSM_INNER, SSM_INNER + G * N], axis=-1)
    dt = jax.nn.softplus(dt_raw.astype(f32) + dt_bias.astype(f32))
    a = -jnp.exp(a_log.astype(f32))
    xh = xs.reshape(b, c, l, G, E, P).astype(f32)
    bm = bm.reshape(b, c, l, G, N).astype(f32)
    cm = cm.reshape(b, c, l, G, N).astype(f32)
    dtc = dt.reshape(b, c, l, G, E)
    xdt = xh * dtc[..., None]
    log_a = jnp.moveaxis(dtc * a.reshape(G, E), 2, -1)
    a_cum = jnp.cumsum(log_a, axis=-1)
    causal = jnp.tril(jnp.ones((l, l), dtype=bool))
    seg = a_cum[..., :, None] - a_cum[..., None, :]
    decay = jnp.exp(jnp.where(causal, seg, -jnp.inf))
    cb = jnp.einsum('bclgn,bcsgn->bcgls', cm, bm)
    y_diag = jnp.einsum('bcgels,bcsgep->bclgep', cb[:, :, :, None] * decay, xdt)
    to_end = jnp.moveaxis(jnp.exp(a_cum[..., -1:] - a_cum), -1, 2)
    states = jnp.einsum('bclgn,bclgep->bcgepn', bm, xdt * to_end[..., None])
    chunk_decay = jnp.exp(a_cum[..., -1])

    def step(carry, inp):
        st, dec = inp
        return carry * dec[..., None, None] + st, carry

    init = jnp.zeros((b, G, E, P, N), f32)
    _, prev = lax.scan(step, init, (jnp.moveaxis(states, 1, 0), jnp.moveaxis(chunk_decay, 1, 0)))
    prev = jnp.moveaxis(prev, 0, 1)
    from_start = jnp.moveaxis(jnp.exp(a_cum), -1, 2)
    y_off = jnp.einsum('bclgn,bcgepn->bclgep', cm, prev) * from_start[..., None]
    y = y_diag + y_off + xh * d_skip.astype(f32).reshape(G, E, 1)
    y = y.reshape(b, s, SSM_INNER) * jax.nn.silu(z.astype(f32))
    yg = y.reshape(b, s, G, SSM_INNER // G)
    yg = yg * lax.rsqrt(jnp.mean(yg * yg, axis=-1, keepdims=True) + NORM_EPS)
    return (yg.reshape(b, s, SSM_INNER) * norm_w.astype(f32)).astype(z.dtype)


def short_conv_mixer(bg, cg, v, gate, conv_w):
    y = bg * causal_depthwise_conv(cg * v, conv_w)
    return y * jax.nn.silu(gate)


def setup_inputs(seed: int = 0) -> dict:
    key = jax.random.key(seed)
    ks = jax.random.split(key, 20)
    f32 = jnp.float32

    def normal(k, shape, scale):
        return jax.random.normal(k, shape, f32) * scale

    x = normal(ks[0], (BATCH, SEQ, D_MODEL), 1.0)
    norm_w = 1.0 + normal(ks[1], (DEPTH, D_MODEL), 0.02)
    w_in = normal(ks[2], (DEPTH, D_MODEL, IN_PROJ_DIM), D_MODEL ** -0.5)
    b_gate = normal(ks[3], (DEPTH, N_BRANCH * D_MODEL), 0.02)
    pool_w = normal(ks[4], (DEPTH, POOL_GROUPS, POOL_GROUP_DIM, POOL_GROUP_DIM), POOL_GROUP_DIM ** -0.5)
    pool_scale = 1.0 + normal(ks[5], (DEPTH, POOL_WIDTH), 0.02)
    ssm_conv_w = normal(ks[6], (DEPTH, SSM_CONV, SSM_CONV_DIM), SSM_CONV ** -0.5)
    ssm_conv_b = normal(ks[7], (DEPTH, SSM_CONV_DIM), 0.02)
    dt0 = jnp.exp(jax.random.uniform(ks[8], (DEPTH, SSM_HEADS), f32) * (np.log(0.1) - np.log(0.001)) + np.log(0.001))
    ssm_dt_bias = dt0 + jnp.log(-jnp.expm1(-dt0))
    ssm_a_log = jnp.log(jax.random.uniform(ks[9], (DEPTH, SSM_HEADS), f32, minval=1.0, maxval=16.0))
    ssm_d = 1.0 + normal(ks[10], (DEPTH, SSM_HEADS), 0.02)
    ssm_norm_w = 1.0 + normal(ks[11], (DEPTH, SSM_INNER), 0.02)
    sc_conv_w = normal(ks[12], (DEPTH, SC_KERNEL, SC_WIDTH), SC_KERNEL ** -0.5)
    w_br_pool = normal(ks[13], (DEPTH, POOL_WIDTH, D_MODEL), POOL_WIDTH ** -0.5)
    w_br_ssm = normal(ks[14], (DEPTH, SSM_INNER, D_MODEL), SSM_INNER ** -0.5)
    w_br_conv = normal(ks[15], (DEPTH, SC_WIDTH, D_MODEL), SC_WIDTH ** -0.5)
    w_out = normal(ks[16], (DEPTH, D_MODEL, D_MODEL), D_MODEL ** -0.5)
    final_norm_w = 1.0 + normal(ks[17], (D_MODEL,), 0.02)
    return {'x': x, 'norm_w': norm_w, 'w_in': w_in, 'b_gate': b_gate, 'pool_w': pool_w, 'pool_scale': pool_scale, 'ssm_conv_w': ssm_conv_w, 'ssm_conv_b': ssm_conv_b, 'ssm_dt_bias': ssm_dt_bias, 'ssm_a_log': ssm_a_log, 'ssm_d': ssm_d, 'ssm_norm_w': ssm_norm_w, 'sc_conv_w': sc_conv_w, 'w_br_pool': w_br_pool, 'w_br_ssm': w_br_ssm, 'w_br_conv': w_br_conv, 'w_out': w_out, 'final_norm_w': final_norm_w}


def reference(x, norm_w, w_in, b_gate, pool_w, pool_scale, ssm_conv_w, ssm_conv_b, ssm_dt_bias, ssm_a_log, ssm_d, ssm_norm_w, sc_conv_w, w_br_pool, w_br_ssm, w_br_conv, w_out, final_norm_w):
    b, s, _ = x.shape
    offsets = np.cumsum(IN_SPLITS)[:-1].tolist()
    for i in range(DEPTH):
        h = rms_norm(x, norm_w[i])
        proj = h @ w_in[i]
        (p_u, p_g, m_z, m_xbc, m_dt, c_b, c_c, c_v, c_g, g_logit) = jnp.split(proj, offsets, axis=-1)
        gates = jax.nn.sigmoid(g_logit + b_gate[i]).reshape(b, s, N_BRANCH, D_MODEL)
        y_a = pool_mixer(p_u, p_g, pool_w[i], pool_scale[i]) @ w_br_pool[i]
        y_b = ssd_mixer(m_xbc, m_z, m_dt, ssm_conv_w[i], ssm_conv_b[i], ssm_dt_bias[i], ssm_a_log[i], ssm_d[i], ssm_norm_w[i]) @ w_br_ssm[i]
        y_c = short_conv_mixer(c_b, c_c, c_v, c_g, sc_conv_w[i]) @ w_br_conv[i]
        merged = gates[:, :, 0] * y_a + gates[:, :, 1] * y_b + gates[:, :, 2] * y_c
        x = x + merged @ w_out[i]
    return rms_norm(x, final_norm_w)
```

```python
import contextlib
import numpy as np
import ml_dtypes
import concourse.bass as bass
import concourse.mybir as mybir
from concourse.bass_utils import run_bass_kernel_spmd

F32 = mybir.dt.float32
BF16 = mybir.dt.bfloat16
AF = mybir.ActivationFunctionType
ALU = mybir.AluOpType

D = 2048
T = 1024
H = 16
TH = T + H
SU_ENG = "dve"
DEPTH = 4
NIN = 17440
EPS = 1e-6
OFF_PU, OFF_PG, OFF_Z, OFF_XS, OFF_B, OFF_C, OFF_DT = 0, 1024, 2048, 4096, 6144, 6656, 7168
OFF_CB, OFF_CC, OFF_CV, OFF_CG, OFF_GL = 7200, 8224, 9248, 10272, 11296
NCOL_L = 232
NCOLS = DEPTH * NCOL_L + 16
C_NW, C_BG, C_PSC, C_SCW, C_SCB, C_SNW, C_CCW = 0, 16, 64, 72, 168, 192, 208
POOL_WINDOWS = (2, 4, 8, 16)


class Tok:
    __slots__ = ("name", "w", "r")

    def __init__(self, name=""):
        self.name = name
        self.w = None
        self.r = []


class _Rec:
    def __getattr__(self, name):
        def f(*a, **k):
            return (name, a, k)
        return f


_REC = _Rec()


class Sched:
    ENGS = ("pe", "act", "dve", "pool", "sp")

    def __init__(self, nc, es, same_engine_sync=True):
        self.nc = nc
        self.es = es
        self.ops = {e: [] for e in self.ENGS}
        self.sems = {}
        self.count = {}
        self.known = {e: {} for e in self.ENGS}
        self.same = same_engine_sync
        for e in self.ENGS:
            self.sems[e] = es.enter_context(nc.semaphore("s_" + e))
            self.count[e] = 0

    def new_sem(self, name):
        key = "x_" + name
        self.sems[key] = self.es.enter_context(self.nc.semaphore(key))
        self.count[key] = 0
        return key

    def _waits(self, eng, reads, writes):
        need = {}

        def add(t):
            if t is None:
                return
            k, v = t
            if need.get(k, 0) < v:
                need[k] = v
        for r in reads:
            add(r.w)
        for w in writes:
            add(w.w)
            for t in w.r:
                add(t)
        out = []
        for k, v in need.items():
            if k == eng and (eng == "pe" or not self.same):
                continue
            if self.known[eng].get(k, 0) >= v:
                continue
            self.known[eng][k] = v
            out.append((k, v))
        return out

    def _commit(self, tok, reads, writes):
        for r in reads:
            r.r.append(tok)
            if len(r.r) > 64:
                mx = {}
                for k, v in r.r:
                    if mx.get(k, 0) < v:
                        mx[k] = v
                r.r = list(mx.items())
        for w in writes:
            w.w = tok
            w.r = []

    def op(self, eng, fn, reads=(), writes=(), inc=True):
        waits = self._waits(eng, reads, writes)
        for k, v in waits:
            if k == eng:
                assert v <= self.count[eng], "self-wait on future milestone"
        if inc:
            self.count[eng] += 1
            tok = (eng, self.count[eng])
        else:
            tok = (eng, self.count[eng] + 1)
        self.ops[eng].append((waits, fn(_REC), eng if inc else None, 1))
        self._commit(tok, reads, writes)
        return tok

    def dma(self, queue, fn, semkey, reads=(), writes=(), inc=16):
        waits = self._waits(queue, reads, writes)
        self.count[semkey] += inc
        tok = (semkey, self.count[semkey])
        self.ops[queue].append((waits, fn(_REC), semkey, inc))
        self._commit(tok, reads, writes)
        return tok

    def drain(self, eng, semkeys):
        self.ops[eng].append(([(k, self.count[k]) for k in semkeys if self.count[k] > 0], None, None, 0))

    def emit(self):
        nc = self.nc
        sems = self.sems

        def run(e, key):
            for waits, fn, inck, incv in self.ops[key]:
                for k, v in waits:
                    e.wait_ge(sems[k], v)
                if fn is None:
                    continue
                ins = getattr(e, fn[0])(*fn[1], **fn[2])
                if inck is not None:
                    ins.then_inc(sems[inck], incv)

        with nc.Block() as block:
            @block.tensor
            def _(e):
                run(e, "pe")

            @block.scalar
            def _(e):
                run(e, "act")

            @block.vector
            def _(e):
                run(e, "dve")

            @block.gpsimd
            def _(e):
                run(e, "pool")

            @block.sync
            def _(e):
                run(e, "sp")


class Buf:
    def __init__(self, t, ntok=1, name=""):
        self.t = t
        self.toks = [Tok(f"{name}{i}") for i in range(ntok)]

    def __getitem__(self, k):
        return self.t[k]


def build_program(NL=DEPTH, final_norm=True, dbg=False):
    nc = bass.Bass("TRN2", target_bir_lowering=False)
    dp = {}

    def din(name, shape, dt=F32):
        dp[name] = nc.dram_tensor(name, list(shape), dt, kind="ExternalInput").ap()
        return dp[name]

    xT_in = din("xT", [128, 16, T])
    xh_in = din("xh", [128, 16, H])
    w_in = din("w_in", [DEPTH, D, NIN])
    w_bp = din("w_br_pool", [DEPTH, 1024, D])
    w_bs = din("w_br_ssm", [DEPTH, D, D])
    w_bc = din("w_br_conv", [DEPTH, 1024, D])
    w_o = din("w_out", [DEPTH, D, D])
    pool_w = din("pool_w", [DEPTH, 4, 256, 256])
    cols_in = din("cols", [128, NCOLS])
    rows_in = din("rows", [128, DEPTH * 96])
    cst_in = din("cst", [128, 4 * 128])
    pc_in = din("pcore", [128, 4 + 96 + 64])
    wsel_in = din("wsel", [4, 3 * 128])
    yT_out = nc.dram_tensor("yT", [128, 16, T], F32, kind="ExternalOutput").ap()
    if dbg:
        dbg_out = nc.dram_tensor("dbg", [128, 8, 16, T], F32, kind="ExternalOutput").ap()
    xspill = nc.dram_tensor("xspill", [128, 16, T], F32).ap()
    pay_h = [nc.dram_tensor(f"pay_h{l}", [128, 256], F32).ap() for l in range(NL)]
    gat_h = [nc.dram_tensor(f"gat_h{l}", [4 * 128, 256], F32).ap() for l in range(NL)]
    pay_s = [[nc.dram_tensor(f"pay_s{l}_{g}", [129, 512], F32).ap() for g in range(4)] for l in range(NL)]
    gat_s = [[nc.dram_tensor(f"gat_s{l}_{g}", [4 * 129, 512], F32).ap() for g in range(4)] for l in range(NL)]

    with contextlib.ExitStack() as es:
        S = Sched(nc, es)
        cur = [16512]

        def alloc(name, shape, dt, at=None, ntok=1):
            esz = 4 if dt == F32 else 2
            n = 1
            for s in shape[1:]:
                n *= s
            nbytes = (n * esz + 31) // 32 * 32
            if at is None:
                off = cur[0]
                cur[0] += nbytes
            else:
                off = at
            t = nc.alloc_sbuf_tensor_at(name, list(shape), dt, offset=off)
            b = Buf(t, ntok, name)
            b.off = off
            b.nbytes = nbytes
            return b

        hT = alloc("hT", [128, 16, TH], BF16, ntok=16)
        W = [alloc(f"W{i}", [128, 16, 512], BF16) for i in range(2)]
        cols = alloc("cols", [128, NCOLS], F32)
        rows = alloc("rows", [128, DEPTH * 96], F32)
        cst = alloc("cst", [128, 4, 128], F32)
        cstb = alloc("cstb", [128, 4, 128], BF16)
        pcore = alloc("pcore", [128, 164], F32)
        wsel = alloc("wsel", [4, 384], F32)
        wdt = alloc("wdt", [128, 16, 32], BF16)
        arow = alloc("arow", [128, 32], F32)
        epst = alloc("epst", [128, 8], F32)
        lat_hl = alloc("lat_hl", [128, 1, 8, 32], BF16)
        lndt = alloc("lndt", [128, 8, 32], F32)
        U0 = cur[0]
        cur[0] += 65536
        U1 = cur[0]
        cur[0] += 32768
        U2 = cur[0]
        cur[0] += 32768
        assert cur[0] <= 229344, cur[0]
        xT = alloc("xTres", [128, 16, T], F32, at=U0, ntok=16)
        xs_tok = alloc("xs_tok", [128, 8, 2048], BF16, at=U0)
        bcT = alloc("bcT", [128, 8, T], BF16, at=U0 + 32768, ntok=8)
        raw = [alloc(f"raw{i}", [128, TH], BF16, at=U0 + 49152 + i * 2080) for i in range(2)]
        acc = [alloc(f"dg{i}", [128, 4, 128], BF16, at=U0 + 49152 + 4160 + i * 1024) for i in range(2)]
        yTg = alloc("yTg", [128, 4, T], F32, at=U0 + 49152, ntok=4)
        pool_out = alloc("pool_out", [128, 8, T], BF16, at=U0, ntok=8)
        conv_out = alloc("conv_out", [128, 8, T], BF16, at=U0 + 16384, ntok=8)
        praw = [alloc(f"praw{i}", [128, TH], F32, at=U0 + 32768 + i * 4160) for i in range(2)]
        pu = alloc("pu", [128, TH], F32, at=U0 + 32768 + 2 * 4160)
        dTt = alloc("dTt", [128, 4, T], BF16, at=U0 + 32768 + 3 * 4160, ntok=4)
        czz = alloc("czz", [128, TH], F32, at=praw[0].off)
        czz.toks = praw[0].toks
        cacc = alloc("cacc", [128, T], F32, at=praw[1].off)
        cacc.toks = praw[1].toks
        assert 32768 + 3 * 4160 + 8192 <= 65536
        gates = alloc("gates", [128, 12, T], BF16, at=U0 + 32768, ntok=12)
        macc = [alloc(f"macc{i}", [128, 512], F32, at=U0 + 32768 + 24576 + i * 2048) for i in range(2)]
        mtmp = [alloc(f"mtmp{i}", [128, 512], F32, at=U0 + 32768 + 24576 + 4096 + i * 2048) for i in range(2)]
        ssm_out = alloc("ssm_out", [128, 16, T], BF16, at=U1, ntok=16)
        sqt = [alloc(f"sqt{i}", [128, TH], BF16, at=U1 + i * 2080) for i in range(2)]
        xhalo = alloc("xhalo", [128, 16, H], F32, at=U1 + 8192)
        ghalo = alloc("ghalo", [128, 4, 256], F32, at=U1 + 12288)
        rstd = alloc("rstd", [128, TH], F32, at=U1 + 16384)
        xsT = [alloc(f"xsT{i}", [128, T], BF16, at=U1 + 28672 + i * 2048) for i in range(2)]
        Fg = alloc("Fg", [128, 512], F32, at=U1 + 26624)
        rstg2 = alloc("rstg2", [128, 512], F32, at=U1 + 24576)
        szt2 = alloc("szt2", [128, 512], BF16, at=U1 + 28672)
        szt2.toks = xsT[0].toks
        u1_users = sqt[0].toks + sqt[1].toks + xhalo.toks + ghalo.toks + rstd.toks + xsT[0].toks + xsT[1].toks + Fg.toks + rstg2.toks
        merged = alloc("merged", [128, 16, T], BF16, at=U2, ntok=16)
        o = [U2]

        def a2(name, shape, dt, ntok=1):
            b = alloc(name, shape, dt, at=o[0], ntok=ntok)
            o[0] += b.nbytes
            return b
        dtt = a2("dtt", [128, 8, 32], F32)
        lat = a2("lat", [128, 8, 32], F32)
        dtte = a2("dtte", [128, 8, 32], F32)
        decb = a2("decb", [128, 8, 32], F32)
        logD = a2("logD", [128, 32], F32)
        coef = a2("coef", [128, 96], F32)
        lD4 = a2("lD4", [4, 32], F32)
        bm_tok = a2("bm_tok", [128, 8, 128], BF16)
        DIg = a2("DIg", [128, 8, 128], BF16)
        Pst = a2("Pst", [128, 512], F32)
        Pb = [a2(f"Pb{i}", [128, 512], BF16) for i in range(2)]
        xdte = [a2(f"xdte{i}", [128, 512], BF16) for i in range(2)]
        rla = a2("rla", [128, 2, 8, 128], BF16)
        CBm2 = [a2(f"CBm{i}", [128, 128], BF16) for i in range(2)]
        decT = a2("decT", [128, 8, 128], BF16)
        Ebc = a2("Ebc", [128, 8, 128], BF16)
        szt = alloc("szt", [128, 512], BF16, at=decT.off)
        szt.toks = decT.toks
        rstg = alloc("rstg", [128, 512], F32, at=Ebc.off)
        rstg.toks = Ebc.toks
        Mp = [a2(f"Mp{i}", [128, 8, 128], BF16) for i in range(2)]
        Cdec = [a2(f"Cdec{i}", [128, 8, 128], BF16) for i in range(2)]
        assert o[0] <= U2 + 32768, (o[0] - U2)
        PS = []
        for i in range(7):
            t = es.enter_context(nc.psum_tensor(f"ps{i}", [128, 512], F32))
            PS.append(Buf(t, 1, f"ps{i}"))
        pst_t = es.enter_context(nc.psum_tensor("pstr", [128, 8, 128], BF16))
        PSTR = Buf(pst_t, 1, "pstr")
        ps_rr = [0]

        def psum():
            b = PS[ps_rr[0] % 5]
            ps_rr[0] += 1
            return b

        sem_w = [S.new_sem("w0"), S.new_sem("w1")]
        sem_x = S.new_sem("xsp")
        sem_xr = [S.new_sem(f"xr{i}") for i in range(16)]
        sem_pay = S.new_sem("pay")
        sem_cc = S.new_sem("cc")
        sem_gin = S.new_sem("gin")
        sem_out = S.new_sem("out")
        sem_wdt = S.new_sem("wdt")

        U_f, L_f, ONE_f, I_f = (cst.t[:, i, :] for i in range(4))
        U_b, L_b, ONE_b, I_b = (cstb.t[:, i, :] for i in range(4))
        CST = cst.toks[0]
        CSTB = cstb.toks[0]

        def col(i):
            return cols.t[:, i:i + 1]

        S.dma("sp", lambda e: e.dma_start(out=cols.t[:], in_=cols_in), S.new_sem("l1"), writes=cols.toks)
        S.dma("sp", lambda e: e.dma_start(out=rows.t[:], in_=rows_in), S.new_sem("l2"), writes=rows.toks)
        S.dma("sp", lambda e: e.dma_start(out=cst.t[:], in_=cst_in.rearrange("p (a b) -> p a b", a=4)), S.new_sem("l3"), writes=cst.toks)
        S.dma("sp", lambda e: e.dma_start(out=pcore.t[:], in_=pc_in), S.new_sem("l4"), writes=pcore.toks)
        S.dma("sp", lambda e: e.dma_start(out=wsel.t[:], in_=wsel_in), S.new_sem("l5"), writes=wsel.toks)
        S.dma("sp", lambda e: e.dma_start(out=xT.t[:], in_=xT_in), sem_x, writes=xT.toks)
        S.dma("sp", lambda e: e.dma_start(out=xhalo.t[:], in_=xh_in), S.new_sem("l6"), writes=xhalo.toks)
        S.op("dve", lambda e: e.tensor_copy(out=cstb.t[:], in_=cst.t[:]), reads=cst.toks, writes=cstb.toks)
        S.op("dve", lambda e: e.memset(epst.t[:], EPS), writes=epst.toks)
        epsc = epst.t[:, 0:1]
        selh = pcore.t[:, 0:4]
        negmask = pcore.t[:, 4:100]
        poolcorr = pcore.t[:, 100:164]

        wstate = {"n": 0}

        def wload(src_aps, kcs):
            b = wstate["n"] % 2
            wstate["n"] += 1
            buf = W[b]
            for i, (src, c0) in enumerate(src_aps):
                ncols = src.shape[1]
                v = src.rearrange("(kc p) n -> p kc n", p=128)
                S.dma("pool", lambda e, v=v, c0=c0, ncols=ncols, buf=buf, kcs=kcs:
                      e.dma_start(out=buf.t[:, 0:kcs, c0:c0 + ncols], in_=v),
                      sem_w[b], writes=buf.toks)
            return buf

        def mm_group(ps, wbuf, m0, kcs, rhs_fn, rhs_toks, ncol, out_cols=None):
            for kc in range(kcs):
                S.op("pe", lambda e, kc=kc: e.matmul(
                    ps.t[:, 0:ncol] if out_cols is None else ps.t[:, out_cols[0]:out_cols[1]],
                    wbuf.t[:, kc, m0:m0 + 128], rhs_fn(kc), start=(kc == 0), stop=(kc == kcs - 1)),
                    reads=wbuf.toks + ([rhs_toks[kc]] if len(rhs_toks) == kcs else rhs_toks), writes=ps.toks, inc=(kc == kcs - 1))

        pe_pending = []

        def flush_pe_pending():
            while pe_pending:
                pe_pending.pop(0)()

        def inproj_chunk(wbuf, m0, halo, consume):
            for half in range(2):
                ps = psum()
                mm_group(ps, wbuf, m0, 16, lambda kc, half=half: hT.t[:, kc, H + half * 512:H + (half + 1) * 512], hT.toks, 512)
                flush_pe_pending()
                consume(half, ps)
            if halo:
                ps = psum()
                mm_group(ps, wbuf, m0, 16, lambda kc: hT.t[:, kc, 0:H], hT.toks, H)
                consume(2, ps)

        for l in range(NL + (1 if final_norm else 0)):
            is_final = (l == NL)
            cb = l * NCOL_L if not is_final else DEPTH * NCOL_L
            psn = [PS[5], PS[6], psum()]
            if l == 0:
                for kc in range(16):
                    sq = sqt[kc % 2]
                    S.op("act", lambda e: e.activation(out=sq.t[:, H:TH], in_=xT.t[:, kc, :], func=AF.Square),
                         reads=[xT.toks[kc]], writes=sq.toks)
                    for part in range(2):
                        lo, hi = (H, H + 512) if part == 0 else (H + 512, TH)
                        S.op("pe", lambda e: e.matmul(
                            psn[part].t[:, 0:hi - lo], ONE_b, sq.t[:, lo:hi], start=(kc == 0), stop=(kc == 15)),
                            reads=sq.toks + [CSTB], writes=psn[part].toks)
            for part in range(2):
                lo, hi = (H, H + 512) if part == 0 else (H + 512, TH)
                S.op("act", lambda e: e.activation(out=rstd.t[:, lo:hi], in_=psn[part].t[:, 0:hi - lo], func=AF.Ln, bias=epsc, scale=1.0 / D),
                     reads=psn[part].toks + epst.toks, writes=rstd.toks)
                S.op("act", lambda e: e.activation(out=rstd.t[:, lo:hi], in_=rstd.t[:, lo:hi], func=AF.Exp, scale=-0.5), reads=rstd.toks, writes=rstd.toks)
            if is_final:
                for kc in range(16):
                    S.op("dve", lambda e: e.scalar_tensor_tensor(
                        out=xT.t[:, kc, :], in0=xT.t[:, kc, :], scalar=col(cb + kc), in1=rstd.t[:, H:TH],
                        op0=ALU.mult, op1=ALU.mult), reads=[xT.toks[kc]] + rstd.toks + cols.toks, writes=[xT.toks[kc]])
                S.dma("sp", lambda e: e.dma_start(out=yT_out, in_=xT.t[:]), sem_out, reads=xT.toks)
                break
            for kc in range(16):
                S.op("dve", lambda e: e.scalar_tensor_tensor(
                    out=hT.t[:, kc, H:TH], in0=xT.t[:, kc, :], scalar=col(cb + C_NW + kc), in1=rstd.t[:, H:TH],
                    op0=ALU.mult, op1=ALU.mult), reads=[xT.toks[kc]] + rstd.toks + cols.toks, writes=[hT.toks[kc]])
            u0_users = (xs_tok.toks + bcT.toks + raw[0].toks + raw[1].toks + acc[0].toks + acc[1].toks + yTg.toks
                        + pool_out.toks + conv_out.toks + praw[0].toks + praw[1].toks + pu.toks + dTt.toks + czz.toks
                        + cacc.toks + gates.toks + macc[0].toks + macc[1].toks + mtmp[0].toks + mtmp[1].toks)
            S.dma("sp", lambda e: e.dma_start(out=xspill, in_=xT.t[:]), sem_x, reads=xT.toks, writes=u0_users)
            u2_users = (dtt.toks + lat.toks + dtte.toks + decb.toks + logD.toks + coef.toks + lD4.toks + bm_tok.toks + DIg.toks
                        + Pst.toks + Pb[0].toks + Pb[1].toks + xdte[0].toks + xdte[1].toks + rla.toks + CBm2[0].toks + CBm2[1].toks
                        + decT.toks + Ebc.toks + Mp[0].toks + Mp[1].toks + Cdec[0].toks + Cdec[1].toks)
            if l > 0:
                xh2 = xhalo.t[:].rearrange("p k h -> p (k h)")
                S.op("dve", lambda e: e.tensor_scalar(out=xh2, in0=ghalo.t[:, 0, :], scalar1=selh[:, 0:1], scalar2=None, op0=ALU.mult),
                     reads=ghalo.toks + pcore.toks, writes=xhalo.toks)
                for jx in range(1, 4):
                    S.op("dve", lambda e: e.scalar_tensor_tensor(out=xh2, in0=ghalo.t[:, jx, :], scalar=selh[:, jx:jx + 1], in1=xh2, op0=ALU.mult, op1=ALU.add),
                         reads=ghalo.toks + pcore.toks + xhalo.toks, writes=xhalo.toks)
            sqh = sqt[0]
            S.op("act", lambda e: e.activation(out=sqh.t[:, 0:256].rearrange("p (k h) -> p k h", k=16), in_=xhalo.t[:], func=AF.Square),
                 reads=xhalo.toks, writes=sqh.toks)
            for kc in range(16):
                S.op("pe", lambda e: e.matmul(psn[2].t[:, 0:H], ONE_b, sqh.t[:, kc * H:(kc + 1) * H], start=(kc == 0), stop=(kc == 15)),
                     reads=sqh.toks + [CSTB], writes=psn[2].toks, inc=(kc == 15))
            S.op("act", lambda e: e.activation(out=rstd.t[:, 0:H], in_=psn[2].t[:, 0:H], func=AF.Ln, bias=epsc, scale=1.0 / D),
                 reads=psn[2].toks + epst.toks, writes=rstd.toks)
            S.op("act", lambda e: e.activation(out=rstd.t[:, 0:H], in_=rstd.t[:, 0:H], func=AF.Exp, scale=-0.5), reads=rstd.toks, writes=rstd.toks)
            for kc in range(16):
                S.op("dve", lambda e: e.scalar_tensor_tensor(
                    out=hT.t[:, kc, 0:H], in0=xhalo.t[:, kc, :], scalar=col(cb + C_NW + kc), in1=rstd.t[:, 0:H],
                    op0=ALU.mult, op1=ALU.mult), reads=xhalo.toks + rstd.toks + cols.toks, writes=[hT.toks[kc]])
            wl = w_in[l]
            S.dma("pool", lambda e, wl=wl: e.dma_start(out=wdt.t[:], in_=wl[:, OFF_DT:OFF_DT + 32].rearrange("(kc p) n -> p kc n", p=128)),
                  sem_wdt, writes=wdt.toks)
            r0 = l * 96
            dtb_row = rows.t[:, r0:r0 + 32]
            alog_row = rows.t[:, r0 + 32:r0 + 64]
            d_row = rows.t[:, r0 + 64:r0 + 96]
            S.op("act", lambda e, alog_row=alog_row: e.activation(out=arow.t[:], in_=alog_row, func=AF.Exp), reads=rows.toks, writes=arow.toks)
            S.op("dve", lambda e: e.tensor_scalar(out=arow.t[:], in0=arow.t[:], scalar1=-1.0, scalar2=None, op0=ALU.mult),
                 reads=arow.toks, writes=arow.toks)

            pend = {"B": None, "C": None}

            def xbc_stageA(wbuf, m0, ch, out_ap, out_toks, ri, post):
                rw = raw[ri % 2]
                dg = acc[ri % 2]

                def consume(part, ps):
                    lo, hi = (H, H + 512) if part == 0 else ((H + 512, TH) if part == 1 else (0, H))
                    S.op("act", lambda e: e.activation(out=rw.t[:, lo:hi], in_=ps.t[:, 0:hi - lo], func=AF.Copy),
                         reads=ps.toks, writes=rw.toks)
                inproj_chunk(wbuf, m0, True, consume)
                wc = cb + C_SCW + ch
                S.op("dve", lambda e: e.tensor_tensor(
                    out=dg.t[:], in0=I_f.unsqueeze(1).broadcast_to([128, 4, 128]),
                    in1=cols.t[:, wc:wc + 73:24].unsqueeze(2).broadcast_to([128, 4, 128]), op=ALU.mult),
                    reads=[CST] + cols.toks, writes=dg.toks)

                def stageB():
                    for half in range(2):
                        ps = psum()
                        for k in range(4):
                            sh = 3 - k
                            lo = H - sh + half * 512
                            S.op("pe", lambda e: e.matmul(ps.t[:, 0:512], dg.t[:, k, :], rw.t[:, lo:lo + 512], start=(k == 0), stop=(k == 3)),
                                 reads=dg.toks + rw.toks, writes=ps.toks, inc=(k == 3))
                        S.op("act", lambda e: e.activation(out=out_ap[:, half * 512:(half + 1) * 512], in_=ps.t[:, 0:512], func=AF.Silu,
                                                           bias=col(cb + C_SCB + ch), scale=1.0),
                             reads=ps.toks + cols.toks, writes=out_toks)
                    return post
                return stageB

            def xbc_step(stageB_new):
                c_new = pend["B"]() if pend["B"] is not None else None
                if pend["C"] is not None:
                    pend["C"]()
                pend["B"] = stageB_new
                pend["C"] = c_new

            def xbc_flush():
                while pend["B"] is not None or pend["C"] is not None:
                    xbc_step(None)

            ri = 0
            for blk in range(2):
                wb = wload([(wl[:, OFF_B + blk * 512:OFF_B + (blk + 1) * 512], 0)], 16)
                for mm in range(4):
                    j = blk * 4 + mm
                    xbc_step(xbc_stageA(wb, mm * 128, 16 + j, bcT.t[:, j, :], [bcT.toks[j]], ri, None))
                    ri += 1
            psd = psum()
            for c in range(8):
                for kc in range(16):
                    S.op("pe", lambda e, c=c, kc=kc: e.matmul(
                        psd.t[:, c * 32:(c + 1) * 32], hT.t[:, kc, H + c * 128:H + (c + 1) * 128], wdt.t[:, kc, :],
                        start=(kc == 0), stop=(kc == 15)), reads=hT.toks + wdt.toks, writes=psd.toks, inc=(kc == 15))
            dtt3 = dtt.t[:]
            S.op("dve", lambda e: e.tensor_tensor(out=dtt.t[:], in0=psd.t[:, 0:256].rearrange("p (c h) -> p c h", c=8),
                                                  in1=dtb_row.unsqueeze(1).broadcast_to([128, 8, 32]), op=ALU.add),
                 reads=psd.toks + rows.toks, writes=dtt.toks)
            S.op("act", lambda e: e.activation(out=dtt.t[:], in_=dtt.t[:], func=AF.Exp), reads=dtt.toks, writes=dtt.toks)
            S.op("act", lambda e: e.activation(out=dtt.t[:], in_=dtt.t[:], func=AF.Ln, bias=1.0, scale=1.0), reads=dtt.toks, writes=dtt.toks)
            S.op("dve", lambda e: e.tensor_tensor(out=lat.t[:], in0=dtt.t[:], in1=arow.t[:].unsqueeze(1).broadcast_to([128, 8, 32]), op=ALU.mult),
                 reads=dtt.toks + arow.toks, writes=lat.toks)
            S.op("dve", lambda e: e.tensor_copy(out=lat_hl.t[:, 0], in_=lat.t[:]), reads=lat.toks, writes=lat_hl.toks)
            S.op("act", lambda e: e.activation(out=lndt.t[:], in_=dtt.t[:], func=AF.Ln), reads=dtt.toks, writes=lndt.toks)
            lat2 = lat.t[:].rearrange("p c h -> p (c h)")
            ps_te = psum()
            S.op("pe", lambda e: e.matmul(ps_te.t[:, 0:256], L_f, lat2, start=True, stop=True), reads=lat.toks + [CST], writes=ps_te.toks)
            ps_tot = psum()
            S.op("pe", lambda e: e.matmul(ps_tot.t[:, 0:256], ONE_f, lat2, start=True, stop=True), reads=lat.toks + [CST], writes=ps_tot.toks)
            ps_ld = psum()
            for c in range(8):
                S.op("pe", lambda e, c=c: e.matmul(ps_ld.t[:, 0:32], ONE_f, lat.t[:, c, :], start=(c == 0), stop=(c == 7)),
                     reads=lat.toks + [CST], writes=ps_ld.toks, inc=(c == 7))
            S.op("act", lambda e: e.activation(out=dtte.t[:].rearrange("p c h -> p (c h)"), in_=ps_te.t[:, 0:256], func=AF.Exp),
                 reads=ps_te.toks, writes=dtte.toks)
            S.op("dve", lambda e: e.tensor_tensor(out=dtte.t[:], in0=dtte.t[:], in1=dtt.t[:], op=ALU.mult), reads=dtte.toks + dtt.toks, writes=dtte.toks)
            S.op("act", lambda e: e.activation(out=decb.t[:].rearrange("p c h -> p (c h)"), in_=ps_tot.t[:, 0:256], func=AF.Exp),
                 reads=ps_tot.toks, writes=decb.toks)
            S.op("act", lambda e: e.activation(out=logD.t[:], in_=ps_ld.t[:, 0:32], func=AF.Copy), reads=ps_ld.toks, writes=logD.toks)

            def bc8(ap2, g):
                return ap2[:, g * 8:(g + 1) * 8].unsqueeze(2).broadcast_to([128, 8, 64])

            def state_update(g, c, Pt, first):
                xd = xdte[c % 2]
                S.op(SU_ENG, lambda e: e.tensor_tensor(
                    out=xd.t[:].rearrange("p (h q) -> p h q", h=8),
                    in0=xs_tok.t[:, c, g * 512:(g + 1) * 512].rearrange("p (h q) -> p h q", h=8),
                    in1=bc8(dtte.t[:, c, :], g), op=ALU.mult), reads=xs_tok.toks + dtte.toks, writes=xd.toks)
                pss = psum()
                S.op("pe", lambda e: e.matmul(pss.t[:, 0:512], bm_tok.t[:, c, :], xd.t[:], start=True, stop=True),
                     reads=bm_tok.toks + xd.toks, writes=pss.toks)
                if first:
                    S.op("act", lambda e: e.activation(out=Pt.t[:], in_=pss.t[:, 0:512], func=AF.Copy), reads=pss.toks, writes=Pt.toks)
                else:
                    S.op(SU_ENG, lambda e: e.tensor_tensor(
                        out=Pt.t[:].rearrange("p (h q) -> p h q", h=8), in0=Pt.t[:].rearrange("p (h q) -> p h q", h=8),
                        in1=bc8(decb.t[:, c, :], g), op=ALU.mult), reads=Pt.toks + decb.toks, writes=Pt.toks)
                    S.op("dve", lambda e: e.tensor_tensor(out=Pt.t[:], in0=pss.t[:, 0:512], in1=Pt.t[:], op=ALU.add),
                         reads=Pt.toks + pss.toks, writes=Pt.toks)

            deferred = []

            def run_deferred(n):
                for _ in range(min(n, len(deferred))):
                    deferred.pop(0)()

            def mk_xs_transposes(j, xt_):
                def f():
                    for c in range(8):
                        S.op("pe", lambda e: e.transpose(PSTR.t[:, c, :], xt_.t[:, c * 128:(c + 1) * 128], I_b),
                             reads=xt_.toks + [CSTB], writes=PSTR.toks, inc=(c == 7))
                    S.op("act", lambda e: e.activation(out=xs_tok.t[:, :, j * 128:(j + 1) * 128], in_=PSTR.t[:], func=AF.Copy),
                         reads=PSTR.toks, writes=xs_tok.toks)
                return f

            def mk_bm_transposes(g):
                def f():
                    for c in range(8):
                        S.op("pe", lambda e: e.transpose(PSTR.t[:, c, :], bcT.t[:, g, c * 128:(c + 1) * 128], I_b),
                             reads=[bcT.toks[g], CSTB], writes=PSTR.toks, inc=(c == 7))
                    S.op("act", lambda e: e.activation(out=bm_tok.t[:], in_=PSTR.t[:], func=AF.Copy), reads=PSTR.toks, writes=bm_tok.toks)
                return f

            cc_toks = [Tok(f"cc{g}") for g in range(4)]

            def mk_exchange(g):
                def f():
                    S.dma("sp", lambda e: e.dma_start(out=pay_s[l][g][0:128, :], in_=Pst.t[:]), sem_pay, reads=Pst.toks)
                    S.dma("sp", lambda e: e.dma_start(out=pay_s[l][g][128:129, :], in_=Pst.t[0:1, :]), sem_pay, reads=Pst.toks)
                    S.ops["sp"].append(([(sem_pay, S.count[sem_pay])], None, None, 0))
                    S.dma("sp", lambda e: e.dma_start(out=pay_s[l][g][128:129, 0:32], in_=logD.t[0:1, :]), sem_pay, reads=logD.toks)
                    pay_wait = (sem_pay, S.count[sem_pay])
                    S.ops["pool"].append(([pay_wait], None, None, 0))
                    S.dma("pool", lambda e: e.collective_compute(
                        "AllGather", ALU.bypass, replica_groups=[[0, 1, 2, 3], [4, 5, 6, 7]],
                        ins=[pay_s[l][g]], outs=[gat_s[l][g]]), sem_cc, writes=[cc_toks[g]], inc=1)
                return f

            for g in range(4):
                wb = wload([(wl[:, OFF_XS + g * 512:OFF_XS + (g + 1) * 512], 0)], 16)
                for mm in range(4):
                    j = g * 4 + mm
                    xt_ = xsT[j % 2]
                    xbc_step(xbc_stageA(wb, mm * 128, j, xt_.t[:], xt_.toks, ri, mk_xs_transposes(j, xt_)))
                    ri += 1
                    run_deferred(3)
                    if mm == 1 and g > 0:
                        deferred.append(mk_bm_transposes(g - 1))
                        for c in range(8):
                            deferred.append(lambda g=g, c=c: state_update(g - 1, c, Pst, c == 0))
                        deferred.append(mk_exchange(g - 1))
            xbc_flush()
            deferred.append(mk_bm_transposes(3))
            for c in range(8):
                deferred.append(lambda c=c: state_update(3, c, Pst, c == 0))
            deferred.append(mk_exchange(3))
            run_deferred(len(deferred))

            rlat = [Tok("rla0"), Tok("rla1")]
            rla.toks.extend(rlat)

            def prep1a(g, c, b):
                tsl = slice(c * 128, (c + 1) * 128)
                pcb = psum()
                S.op("pe", lambda e: e.matmul(pcb.t[:, 0:128], bcT.t[:, g, tsl], bcT.t[:, 4 + g, tsl], start=True, stop=True),
                     reads=[bcT.toks[g], bcT.toks[4 + g]], writes=pcb.toks)
                S.op("dve", lambda e: e.tensor_tensor(
                    out=rla.t[:, b], in0=U_b.unsqueeze(1).broadcast_to([128, 8, 128]),
                    in1=lat_hl.t[:, 0, c, g * 8:(g + 1) * 8].unsqueeze(2).broadcast_to([128, 8, 128]), op=ALU.mult),
                    reads=[CSTB] + lat_hl.toks, writes=[rlat[b]] + ([rla.toks[0]] if b == 0 else []))
                S.op("dve", lambda e: e.tensor_tensor(out=CBm2[b].t[:], in0=pcb.t[:, 0:128], in1=U_f, op=ALU.mult),
                     reads=pcb.toks + [CST], writes=CBm2[b].toks)

            def prep1b(g, c, b):
                pq = {}
                for (nm, lhs) in (("seg", L_b), ("abc", ONE_b)):
                    for hh in range(2):
                        pseg = psum()
                        rsl = rla.t[:, b, hh * 4:(hh + 1) * 4, :].rearrange("p h l -> p (h l)")
                        S.op("pe", lambda e: e.matmul(pseg.t[:, 0:512], lhs, rsl, start=True, stop=True),
                             reads=[rlat[b], CSTB], writes=pseg.toks)
                        pq[(nm, hh)] = pseg
                for hl in range(8):
                    h = g * 8 + hl
                    ps_ = pq[("seg", hl // 4)]
                    S.op("act", lambda e: e.activation(out=decT.t[:, hl, :], in_=ps_.t[:, (hl % 4) * 128:(hl % 4 + 1) * 128], func=AF.Exp,
                                                       bias=lndt.t[:, c, h:h + 1], scale=1.0),
                         reads=ps_.toks + lndt.toks, writes=decT.toks)
                for hh in range(2):
                    ps_ = pq[("abc", hh)]
                    S.op("act", lambda e: e.activation(out=Ebc.t[:, hh * 4:(hh + 1) * 4, :].rearrange("p h l -> p (h l)"), in_=ps_.t[:, 0:512], func=AF.Exp),
                         reads=ps_.toks, writes=Ebc.toks)

            def prep2(g, c, b):
                tsl = slice(c * 128, (c + 1) * 128)
                S.op("dve", lambda e: e.tensor_tensor(out=Mp[b].t[:], in0=decT.t[:], in1=CBm2[b].t[:].unsqueeze(1).broadcast_to([128, 8, 128]), op=ALU.mult),
                     reads=decT.toks + CBm2[b].toks, writes=Mp[b].toks)
                S.op("dve", lambda e: e.tensor_tensor(
                    out=Cdec[b].t[:], in0=Ebc.t[:], in1=bcT.t[:, 4 + g, tsl].unsqueeze(1).broadcast_to([128, 8, 128]), op=ALU.mult),
                    reads=Ebc.toks + [bcT.toks[4 + g]], writes=Cdec[b].toks)

            def early_start(g):
                S.op("dve", lambda e: e.tensor_tensor(
                    out=DIg.t[:], in0=I_f.unsqueeze(1).broadcast_to([128, 8, 128]),
                    in1=d_row[:, g * 8:(g + 1) * 8].unsqueeze(2).broadcast_to([128, 8, 128]), op=ALU.mult),
                    reads=[CST] + rows.toks, writes=DIg.toks)
                mk_bm_transposes(g)()
                prep1a(g, 0, 0)
                prep1a(g, 1, 1)
                prep1b(g, 0, 0)

            for g in range(4):
                gv = gat_s[l][g].rearrange("(r q) f -> q r f", r=4)
                if g == 0:
                    early_start(0)
                szt_g, rstg_g = (szt2, rstg2) if g < 3 else (szt, rstg)
                if g == 0:
                    S.dma("sp", lambda e: e.dma_start(out=lD4.t[:], in_=gv[128, :, 0:32]), sem_gin, reads=[cc_toks[g]], writes=lD4.toks)
                    psc_ = psum()
                    for jx in range(3):
                        S.op("pe", lambda e: e.matmul(psc_.t[:, jx * 32:(jx + 1) * 32], wsel.t[:, jx * 128:(jx + 1) * 128], lD4.t[:],
                                                      start=True, stop=True), reads=wsel.toks + lD4.toks, writes=psc_.toks)
                    S.op("dve", lambda e: e.tensor_tensor(out=coef.t[:], in0=psc_.t[:, 0:96], in1=negmask, op=ALU.add),
                         reads=psc_.toks + pcore.toks, writes=coef.toks)
                    S.op("act", lambda e: e.activation(out=coef.t[:], in_=coef.t[:], func=AF.Exp), reads=coef.toks, writes=coef.toks)
                for jx in range(3):
                    S.dma("sp", lambda e: e.dma_start(out=Fg.t[:], in_=gv[0:128, jx, :]), sem_gin, reads=[cc_toks[g]], writes=Fg.toks)
                    cf = coef.t[:, jx * 32 + g * 8: jx * 32 + (g + 1) * 8].unsqueeze(2).broadcast_to([128, 8, 64])
                    fj = Fg.t[:].rearrange("p (h q) -> p h q", h=8)
                    if jx == 0:
                        S.op("dve", lambda e: e.tensor_tensor(out=Pst.t[:].rearrange("p (h q) -> p h q", h=8), in0=fj, in1=cf, op=ALU.mult),
                             reads=Fg.toks + coef.toks, writes=Pst.toks)
                    else:
                        S.op("dve", lambda e: e.tensor_tensor(out=fj, in0=fj, in1=cf, op=ALU.mult),
                             reads=Fg.toks + coef.toks, writes=Fg.toks)
                        S.op("dve", lambda e: e.tensor_tensor(out=Pst.t[:], in0=Pst.t[:], in1=Fg.t[:], op=ALU.add),
                             reads=Fg.toks + Pst.toks, writes=Pst.toks)
                S.op("act", lambda e: e.activation(out=Pb[0].t[:], in_=Pst.t[:], func=AF.Copy), reads=Pst.toks, writes=Pb[0].toks)
                prep2(g, 0, 0)
                for c in range(8):
                    pbc = Pb[c % 2]
                    b = c % 2
                    tsl = slice(c * 128, (c + 1) * 128)
                    if c < 7:
                        prep1b(g, c + 1, (c + 1) % 2)
                        if c < 6:
                            prep1a(g, c + 2, c % 2)
                        state_update(g, c, Pst, False)
                        S.op("act", lambda e: e.activation(out=Pb[(c + 1) % 2].t[:], in_=Pst.t[:], func=AF.Copy),
                             reads=Pst.toks, writes=Pb[(c + 1) % 2].toks)
                        prep2(g, c + 1, (c + 1) % 2)
                    py = psum()
                    for hl in range(8):
                        jj, hh = hl // 2, hl % 2
                        ch0 = (g * 8 + hl) * 64
                        outp = py.t[hh * 64:(hh + 1) * 64, jj * 128:(jj + 1) * 128]
                        tp = (0, hh * 64)
                        S.op("pe", lambda e: e.matmul(
                            outp, xs_tok.t[:, c, ch0:ch0 + 64], Mp[b].t[:, hl, :], start=True, stop=False, tile_position=tp),
                            reads=xs_tok.toks + Mp[b].toks, writes=py.toks, inc=False)
                        S.op("pe", lambda e: e.matmul(
                            outp, xs_tok.t[:, c, ch0:ch0 + 64], DIg.t[:, hl, :], start=False, stop=False, tile_position=tp),
                            reads=xs_tok.toks + DIg.toks, writes=py.toks, inc=False)
                        S.op("pe", lambda e: e.matmul(
                            outp, pbc.t[:, hl * 64:(hl + 1) * 64], Cdec[b].t[:, hl, :], start=False, stop=True, tile_position=tp),
                            reads=pbc.toks + Cdec[b].toks, writes=py.toks, inc=(hl == 7))
                    S.op("dve", lambda e: e.tensor_copy(
                        out=yTg.t[:, :, tsl], in_=py.t[:, 0:512].rearrange("p (j l) -> p j l", j=4)),
                        reads=py.toks, writes=yTg.toks)
                wb = wload([(wl[:, OFF_Z + g * 512:OFF_Z + (g + 1) * 512], 0)], 16)
                psg = [PS[5], PS[6]]
                for mm in range(4):
                    j = g * 4 + mm

                    def consume(part, ps, mm=mm):
                        sl = slice(part * 512, (part + 1) * 512)
                        S.op("act", lambda e: e.activation(out=szt_g.t[:], in_=ps.t[:, 0:512], func=AF.Silu), reads=ps.toks, writes=szt_g.toks)
                        S.op("dve", lambda e: e.tensor_tensor(out=yTg.t[:, mm, sl], in0=yTg.t[:, mm, sl], in1=szt_g.t[:], op=ALU.mult),
                             reads=szt_g.toks + yTg.toks, writes=yTg.toks)
                        S.op("act", lambda e: e.activation(out=szt_g.t[:], in_=yTg.t[:, mm, sl], func=AF.Square), reads=yTg.toks + szt_g.toks, writes=szt_g.toks)
                        pe_pending.append(lambda: S.op("pe", lambda e: e.matmul(psg[part].t[:, 0:512], ONE_b, szt_g.t[:], start=(mm == 0), stop=(mm == 3)),
                                                       reads=szt_g.toks + [CSTB], writes=psg[part].toks))
                    inproj_chunk(wb, mm * 128, False, consume)
                flush_pe_pending()
                if g < 3:
                    early_start(g + 1)
                for part in range(2):
                    sl = slice(part * 512, (part + 1) * 512)
                    S.op("act", lambda e: e.activation(out=rstg_g.t[:], in_=psg[part].t[:, 0:512], func=AF.Ln, bias=epsc, scale=1.0 / 512),
                         reads=psg[part].toks + epst.toks, writes=rstg_g.toks)
                    S.op("act", lambda e: e.activation(out=rstg_g.t[:], in_=rstg_g.t[:], func=AF.Exp, scale=-0.5), reads=rstg_g.toks, writes=rstg_g.toks)
                    for mm in range(4):
                        j = g * 4 + mm
                        S.op("dve", lambda e: e.scalar_tensor_tensor(
                            out=ssm_out.t[:, j, sl], in0=yTg.t[:, mm, sl], scalar=col(cb + C_SNW + j), in1=rstg_g.t[:], op0=ALU.mult, op1=ALU.mult),
                            reads=yTg.toks + rstg_g.toks + cols.toks, writes=[ssm_out.toks[j]] + u1_users)

            for blk in range(2):
                wb = wload([(wl[:, OFF_PG + blk * 512:OFF_PG + (blk + 1) * 512], 0)], 16)
                for mm in range(4):
                    j = blk * 4 + mm

                    def consume(part, ps, j=j):
                        sl = slice(part * 512, (part + 1) * 512)
                        S.op("act", lambda e: e.activation(out=pool_out.t[:, j, sl], in_=ps.t[:, 0:512], func=AF.Silu),
                             reads=ps.toks, writes=[pool_out.toks[j]])
                    inproj_chunk(wb, mm * 128, False, consume)
            pwv = pool_w[l].rearrange("g (kc p) d -> p (g kc) d", p=128)
            for blk in range(2):
                wb = wload([(wl[:, OFF_PU + blk * 512:OFF_PU + (blk + 1) * 512], 0)], 16)
                for mm in range(4):
                    j = blk * 4 + mm
                    gp = j // 2
                    win = POOL_WINDOWS[gp]

                    def consume(part, ps):
                        lo, hi = (H, H + 512) if part == 0 else ((H + 512, TH) if part == 1 else (0, H))
                        S.op("act", lambda e: e.activation(out=pu.t[:, lo:hi], in_=ps.t[:, 0:hi - lo], func=AF.Copy), reads=ps.toks, writes=pu.toks)
                    inproj_chunk(wb, mm * 128, True, consume)
                    src = pu
                    sh = 1
                    k = 0
                    while sh < win:
                        dst = praw[k % 2]
                        lo = 2 * sh - 1
                        S.op("dve", lambda e, src=src, dst=dst, sh=sh, lo=lo: e.tensor_tensor(
                            out=dst.t[:, lo:TH], in0=src.t[:, lo:TH], in1=src.t[:, lo - sh:TH - sh], op=ALU.add),
                            reads=src.toks, writes=dst.toks)
                        src = dst
                        sh *= 2
                        k += 1
                    S.op("dve", lambda e, src=src, gp=gp: e.tensor_tensor(out=src.t[:, H:2 * H], in0=src.t[:, H:2 * H], in1=poolcorr[:, gp * 16:(gp + 1) * 16], op=ALU.mult),
                         reads=src.toks + pcore.toks, writes=src.toks)
                    S.op("dve", lambda e, src=src, win=win, j=j: e.scalar_tensor_tensor(
                        out=dTt.t[:, j % 4, :], in0=src.t[:, H:TH], scalar=1.0 / win, in1=pu.t[:, H:TH], op0=ALU.mult, op1=ALU.subtract),
                        reads=src.toks + pu.toks, writes=[dTt.toks[j % 4]])
                    if j % 2 == 1:
                        if j == 1:
                            wpw = wload([(pool_w[l].rearrange("g k d -> (g k) d")[0:1024, :], 0)], 8)

                        def mk_group(gp=gp, wpw=wpw):
                            def f():
                                d0 = (2 * gp) % 4
                                for dd in range(2):
                                    jo = gp * 2 + dd
                                    for part in range(2):
                                        ps = psum()
                                        sl = slice(part * 512, (part + 1) * 512)
                                        for k2 in range(2):
                                            S.op("pe", lambda e: e.matmul(
                                                ps.t[:, 0:512], wpw.t[:, gp * 2 + k2, dd * 128:(dd + 1) * 128], dTt.t[:, d0 + k2, sl],
                                                start=(k2 == 0), stop=(k2 == 1)), reads=wpw.toks + [dTt.toks[d0], dTt.toks[d0 + 1]], writes=ps.toks, inc=(k2 == 1))
                                        S.op("dve", lambda e: e.scalar_tensor_tensor(
                                            out=pool_out.t[:, jo, sl], in0=ps.t[:, 0:512], scalar=col(cb + C_PSC + jo), in1=pool_out.t[:, jo, sl],
                                            op0=ALU.mult, op1=ALU.mult), reads=ps.toks + [pool_out.toks[jo]] + cols.toks, writes=[pool_out.toks[jo]])
                            return f
                        pe_pending.append(mk_group())
            flush_pe_pending()

            for (off, kind) in ((OFF_CG, "g"), (OFF_CB, "b")):
                for blk in range(2):
                    wb = wload([(wl[:, off + blk * 512:off + (blk + 1) * 512], 0)], 16)
                    for mm in range(4):
                        j = blk * 4 + mm

                        def consume(part, ps, j=j, kind=kind):
                            sl = slice(part * 512, (part + 1) * 512)
                            if kind == "g":
                                S.op("act", lambda e: e.activation(out=conv_out.t[:, j, sl], in_=ps.t[:, 0:512], func=AF.Silu),
                                     reads=ps.toks, writes=[conv_out.toks[j]])
                            else:
                                S.op("dve", lambda e: e.tensor_tensor(out=conv_out.t[:, j, sl], in0=ps.t[:, 0:512], in1=conv_out.t[:, j, sl], op=ALU.mult),
                                     reads=ps.toks + [conv_out.toks[j]], writes=[conv_out.toks[j]])
                        inproj_chunk(wb, mm * 128, False, consume)
            for pr in range(4):
                j0 = pr * 2
                wb = wload([(wl[:, OFF_CC + j0 * 128:OFF_CC + (j0 + 2) * 128], 0), (wl[:, OFF_CV + j0 * 128:OFF_CV + (j0 + 2) * 128], 256)], 16)
                for mm in range(2):
                    j = j0 + mm

                    def consume_c(part, ps):
                        lo, hi = (H, H + 512) if part == 0 else ((H + 512, TH) if part == 1 else (0, H))
                        S.op("act", lambda e: e.activation(out=pu.t[:, lo:hi], in_=ps.t[:, 0:hi - lo], func=AF.Copy), reads=ps.toks, writes=pu.toks)
                    inproj_chunk(wb, mm * 128, True, consume_c)

                    def consume_v(part, ps):
                        lo, hi = (H, H + 512) if part == 0 else ((H + 512, TH) if part == 1 else (0, H))
                        S.op("dve", lambda e: e.tensor_tensor(out=czz.t[:, lo:hi], in0=ps.t[:, 0:hi - lo], in1=pu.t[:, lo:hi], op=ALU.mult),
                             reads=ps.toks + pu.toks, writes=czz.toks)
                    inproj_chunk(wb, 256 + mm * 128, True, consume_v)
                    wc = cb + C_CCW
                    S.op("dve", lambda e, j=j, wc=wc: e.tensor_scalar(out=cacc.t[:], in0=czz.t[:, H:TH], scalar1=col(wc + 2 * 8 + j), scalar2=None, op0=ALU.mult),
                         reads=czz.toks + cols.toks, writes=cacc.toks)
                    for k in (1, 0):
                        sh = 2 - k
                        S.op("dve", lambda e, k=k, sh=sh, j=j, wc=wc: e.scalar_tensor_tensor(
                            out=cacc.t[:], in0=czz.t[:, H - sh:TH - sh], scalar=col(wc + k * 8 + j), in1=cacc.t[:], op0=ALU.mult, op1=ALU.add),
                            reads=czz.toks + cacc.toks + cols.toks, writes=cacc.toks)
                    S.op("dve", lambda e, j=j: e.tensor_tensor(out=conv_out.t[:, j, :], in0=cacc.t[:], in1=conv_out.t[:, j, :], op=ALU.mult),
                         reads=cacc.toks + [conv_out.toks[j]], writes=[conv_out.toks[j]])

            pc_temps = praw[0].toks + praw[1].toks + pu.toks + dTt.toks
            for mb in range(4):
                for k in range(3):
                    wb = wload([(wl[:, OFF_GL + k * 2048 + mb * 512:OFF_GL + k * 2048 + (mb + 1) * 512], 0)], 16)
                    for mm in range(4):
                        m = mb * 4 + mm
                        gi = k * 4 + mm

                        def consume(part, ps, gi=gi, k=k, m=m):
                            sl = slice(part * 512, (part + 1) * 512)
                            S.op("act", lambda e: e.activation(out=gates.t[:, gi, sl], in_=ps.t[:, 0:512], func=AF.Sigmoid,
                                                               bias=col(cb + C_BG + k * 16 + m), scale=1.0),
                                 reads=ps.toks + cols.toks, writes=[gates.toks[gi]] + pc_temps)
                        inproj_chunk(wb, mm * 128, False, consume)
                wbp = wload([(w_bp[l][:, mb * 512:(mb + 1) * 512], 0)], 8)
                for k, (wsrc, kcs, src_buf) in enumerate(((w_bp, 8, pool_out), (w_bs, 16, ssm_out), (w_bc, 8, conv_out))):
                    if k == 0:
                        wbk = wbp
                    else:
                        wbk = wload([(wsrc[l][:, mb * 512:(mb + 1) * 512], 0)], kcs)
                    for mm in range(4):
                        m = mb * 4 + mm
                        gi = k * 4 + mm
                        for part in range(2):
                            sl = slice(part * 512, (part + 1) * 512)
                            ps = psum()
                            mm_group(ps, wbk, mm * 128, kcs, lambda kc, sl=sl, src_buf=src_buf: src_buf.t[:, kc, sl], src_buf.toks, 512)
                            S.op("dve", lambda e, gi=gi, sl=sl, ps=ps: e.tensor_tensor(out=gates.t[:, gi, sl], in0=ps.t[:, 0:512], in1=gates.t[:, gi, sl], op=ALU.mult),
                                 reads=ps.toks + [gates.toks[gi]], writes=[gates.toks[gi]])
                for mm in range(4):
                    m = mb * 4 + mm
                    for part in range(2):
                        sl = slice(part * 512, (part + 1) * 512)
                        ma = macc[part]
                        S.op("dve", lambda e, mm=mm, sl=sl, ma=ma: e.tensor_tensor(out=ma.t[:], in0=gates.t[:, mm, sl], in1=gates.t[:, 4 + mm, sl], op=ALU.add),
                             reads=[gates.toks[mm], gates.toks[4 + mm]], writes=ma.toks)
                        S.op("dve", lambda e, mm=mm, sl=sl, ma=ma, m=m: e.tensor_tensor(out=merged.t[:, m, sl], in0=ma.t[:], in1=gates.t[:, 8 + mm, sl], op=ALU.add),
                             reads=ma.toks + [gates.toks[8 + mm]], writes=[merged.toks[m]] + u2_users)

            for n in range(16):
                S.dma("sp", lambda e: e.dma_start(out=xT.t[:, n, :], in_=xspill[:, n, :]), sem_xr[n], reads=[],
                      writes=[xT.toks[n]] + (u0_users if n == 0 else []))
            psn_c = [PS[5], PS[6]]
            for nb in range(4):
                wb = wload([(w_o[l][:, nb * 512:(nb + 1) * 512], 0)], 16)
                for mm in range(4):
                    n = nb * 4 + mm
                    for part in range(2):
                        sl = slice(part * 512, (part + 1) * 512)
                        ps = psum()
                        mm_group(ps, wb, mm * 128, 16, lambda kc, sl=sl: merged.t[:, kc, sl], merged.toks, 512)
                        if part == 0:
                            flush_pe_pending()
                        S.op("dve", lambda e: e.tensor_tensor(out=xT.t[:, n, sl], in0=ps.t[:, 0:512], in1=xT.t[:, n, sl], op=ALU.add),
                             reads=ps.toks + [xT.toks[n]], writes=[xT.toks[n]])
                    sq = sqt[n % 2]
                    S.op("act", lambda e: e.activation(out=sq.t[:, H:TH], in_=xT.t[:, n, :], func=AF.Square),
                         reads=[xT.toks[n]], writes=sq.toks)

                    def mk(n=n, sq=sq):
                        def f():
                            for part in range(2):
                                lo, hi = (H, H + 512) if part == 0 else (H + 512, TH)
                                S.op("pe", lambda e: e.matmul(psn_c[part].t[:, 0:512], ONE_b, sq.t[:, lo:hi], start=(n == 0), stop=(n == 15)),
                                     reads=sq.toks + [CSTB], writes=psn_c[part].toks)
                        return f
                    pe_pending.append(mk())
            flush_pe_pending()
            if dbg and l == 0:
                pass
            if l + 1 < NL:
                S.dma("sp", lambda e: e.dma_start(out=pay_h[l].rearrange("p (k h) -> p k h", k=16), in_=xT.t[:, :, T - H:T]), sem_pay, reads=xT.toks)
                pay_wait = (sem_pay, S.count[sem_pay])
                S.ops["pool"].append(([pay_wait], None, None, 0))
                cch = Tok("cch")
                S.dma("pool", lambda e: e.collective_compute(
                    "AllGather", ALU.bypass, replica_groups=[[0, 1, 2, 3], [4, 5, 6, 7]],
                    ins=[pay_h[l]], outs=[gat_h[l]]), sem_cc, writes=[cch], inc=1)
                S.dma("sp", lambda e: e.dma_start(out=ghalo.t[:], in_=gat_h[l].rearrange("(r p) f -> p r f", r=4)), sem_gin,
                      reads=[cch], writes=ghalo.toks)
            mr = [t for mt in merged.toks for t in mt.r]
            for tk in u2_users:
                tk.r.extend(mr)
            sr = [t for mt in ssm_out.toks for t in mt.r]
            for tk in u1_users:
                tk.r.extend(sr)
        if not final_norm:
            S.dma("sp", lambda e: e.dma_start(out=yT_out, in_=xT.t[:]), sem_out, reads=xT.toks)
        S.drain("sp", [sem_out, sem_x, sem_pay, sem_gin])
        S.emit()
    return nc


def _cols_pack(v):
    v = np.asarray(v, np.float32).reshape(-1, 128)
    return v.T


def prep_inputs(inputs):
    f32 = np.float32
    x = np.asarray(inputs["x"], f32)
    cols = np.zeros((128, NCOLS), f32)
    for l in range(DEPTH):
        b = l * NCOL_L
        cols[:, b + C_NW:b + C_NW + 16] = _cols_pack(inputs["norm_w"][l])
        cols[:, b + C_BG:b + C_BG + 48] = _cols_pack(inputs["b_gate"][l])
        cols[:, b + C_PSC:b + C_PSC + 8] = _cols_pack(inputs["pool_scale"][l])
        for k in range(4):
            cols[:, b + C_SCW + k * 24:b + C_SCW + (k + 1) * 24] = _cols_pack(inputs["ssm_conv_w"][l][k])
        cols[:, b + C_SCB:b + C_SCB + 24] = _cols_pack(inputs["ssm_conv_b"][l])
        cols[:, b + C_SNW:b + C_SNW + 16] = _cols_pack(inputs["ssm_norm_w"][l])
        for k in range(3):
            cols[:, b + C_CCW + k * 8:b + C_CCW + (k + 1) * 8] = _cols_pack(inputs["sc_conv_w"][l][k])
    cols[:, DEPTH * NCOL_L:DEPTH * NCOL_L + 16] = _cols_pack(inputs["final_norm_w"])
    rows = np.zeros((DEPTH, 96), f32)
    for l in range(DEPTH):
        rows[l, 0:32] = inputs["ssm_dt_bias"][l]
        rows[l, 32:64] = inputs["ssm_a_log"][l]
        rows[l, 64:96] = inputs["ssm_d"][l]
    rows = np.ascontiguousarray(np.broadcast_to(rows.reshape(1, -1), (128, DEPTH * 96)))
    i = np.arange(128)
    U = (i[:, None] <= i[None, :]).astype(f32)
    L = (i[:, None] > i[None, :]).astype(f32)
    cst = np.concatenate([U, L, np.ones((128, 128), f32), np.eye(128, dtype=f32)], axis=1)
    shared = {
        "w_in": np.asarray(inputs["w_in"], f32), "w_br_pool": np.asarray(inputs["w_br_pool"], f32),
        "w_br_ssm": np.asarray(inputs["w_br_ssm"], f32), "w_br_conv": np.asarray(inputs["w_br_conv"], f32),
        "w_out": np.asarray(inputs["w_out"], f32), "pool_w": np.asarray(inputs["pool_w"], f32),
        "cols": cols, "rows": rows, "cst": cst,
    }
    in_maps = []
    for r in range(8):
        b, q = r // 4, r % 4
        xs = x[b, q * T:(q + 1) * T, :]
        xT = np.ascontiguousarray(xs.T.reshape(16, 128, T).transpose(1, 0, 2))
        if q == 0:
            xh = np.zeros((128, 16, H), f32)
        else:
            hh = x[b, q * T - H:q * T, :]
            xh = np.ascontiguousarray(hh.T.reshape(16, 128, H).transpose(1, 0, 2))
        pc = np.zeros((128, 164), f32)
        if q > 0:
            pc[:, q - 1] = 1.0
        negmask = np.zeros((3, 32), f32)
        for j in range(3):
            if j >= q:
                negmask[j, :] = -30000.0
        pc[:, 4:100] = negmask.reshape(1, 96)
        corr = np.ones((4, 16), f32)
        if q == 0:
            for g, win in enumerate(POOL_WINDOWS):
                for t in range(16):
                    corr[g, t] = win / min(t + 1, win)
        pc[:, 100:164] = corr.reshape(1, 64)
        wsel = np.zeros((4, 3, 128), f32)
        for j in range(3):
            for i_ in range(4):
                if j < i_ < q:
                    wsel[i_, j, :] = 1.0
        m = dict(shared)
        m.update({"xT": xT, "xh": xh, "pcore": pc, "wsel": wsel.reshape(4, 384)})
        in_maps.append(m)
    return in_maps


def assemble(results, key="yT"):
    out = np.zeros((2, 4 * T, D), np.float32)
    for r in range(8):
        b, q = r // 4, r % 4
        yT = np.asarray(results[r][key])
        out[b, q * T:(q + 1) * T, :] = yT.transpose(1, 0, 2).reshape(D, T).T
    return out


_NC_CACHE = {}


def kernel(**inputs):
    in_maps = prep_inputs(inputs)
    if "nc" not in _NC_CACHE:
        _NC_CACHE["nc"] = build_program()
    res = run_bass_kernel_spmd(_NC_CACHE["nc"], in_maps, core_ids=list(range(8)))
    return assemble(res.results)
```

```python
import contextlib
import numpy as np
import ml_dtypes
import concourse.bass as bass
import concourse.mybir as mybir
from concourse.bass_utils import run_bass_kernel_spmd

F32 = mybir.dt.float32
BF16 = mybir.dt.bfloat16
AF = mybir.ActivationFunctionType
ALU = mybir.AluOpType

D = 2048
T = 1024
H = 16
TH = T + H
SU_ENG = "dve"
DEPTH = 4
NIN = 17440
EPS = 1e-6
OFF_PU, OFF_PG, OFF_Z, OFF_XS, OFF_B, OFF_C, OFF_DT = 0, 1024, 2048, 4096, 6144, 6656, 7168
OFF_CB, OFF_CC, OFF_CV, OFF_CG, OFF_GL = 7200, 8224, 9248, 10272, 11296
NCOL_L = 232
NCOLS = DEPTH * NCOL_L + 16
C_NW, C_BG, C_PSC, C_SCW, C_SCB, C_SNW, C_CCW = 0, 16, 64, 72, 168, 192, 208
POOL_WINDOWS = (2, 4, 8, 16)


class Tok:
    __slots__ = ("name", "w", "r")

    def __init__(self, name=""):
        self.name = name
        self.w = None
        self.r = []


class _Rec:
    def __getattr__(self, name):
        def f(*a, **k):
            return (name, a, k)
        return f


_REC = _Rec()


class Sched:
    ENGS = ("pe", "act", "dve", "pool", "sp")

    def __init__(self, nc, es, same_engine_sync=True):
        self.nc = nc
        self.es = es
        self.ops = {e: [] for e in self.ENGS}
        self.sems = {}
        self.count = {}
        self.known = {e: {} for e in self.ENGS}
        self.same = same_engine_sync
        for e in self.ENGS:
            self.sems[e] = es.enter_context(nc.semaphore("s_" + e))
            self.count[e] = 0

    def new_sem(self, name):
        key = "x_" + name
        self.sems[key] = self.es.enter_context(self.nc.semaphore(key))
        self.count[key] = 0
        return key

    def _waits(self, eng, reads, writes):
        need = {}

        def add(t):
            if t is None:
                return
            k, v = t
            if need.get(k, 0) < v:
                need[k] = v
        for r in reads:
            add(r.w)
        for w in writes:
            add(w.w)
            for t in w.r:
                add(t)
        out = []
        for k, v in need.items():
            if k == eng and (eng == "pe" or not self.same):
                continue
            if self.known[eng].get(k, 0) >= v:
                continue
            self.known[eng][k] = v
            out.append((k, v))
        return out

    def _commit(self, tok, reads, writes):
        for r in reads:
            r.r.append(tok)
            if len(r.r) > 64:
                mx = {}
                for k, v in r.r:
                    if mx.get(k, 0) < v:
                        mx[k] = v
                r.r = list(mx.items())
        for w in writes:
            w.w = tok
            w.r = []

    def op(self, eng, fn, reads=(), writes=(), inc=True):
        waits = self._waits(eng, reads, writes)
        for k, v in waits:
            if k == eng:
                assert v <= self.count[eng], "self-wait on future milestone"
        if inc:
            self.count[eng] += 1
            tok = (eng, self.count[eng])
        else:
            tok = (eng, self.count[eng] + 1)
        self.ops[eng].append((waits, fn(_REC), eng if inc else None, 1))
        self._commit(tok, reads, writes)
        return tok

    def dma(self, queue, fn, semkey, reads=(), writes=(), inc=16):
        waits = self._waits(queue, reads, writes)
        self.count[semkey] += inc
        tok = (semkey, self.count[semkey])
        self.ops[queue].append((waits, fn(_REC), semkey, inc))
        self._commit(tok, reads, writes)
        return tok

    def drain(self, eng, semkeys):
        self.ops[eng].append(([(k, self.count[k]) for k in semkeys if self.count[k] > 0], None, None, 0))

    def emit(self):
        nc = self.nc
        sems = self.sems

        def run(e, key):
            for waits, fn, inck, incv in self.ops[key]:
                for k, v in waits:
                    e.wait_ge(sems[k], v)
                if fn is None:
                    continue
                ins = getattr(e, fn[0])(*fn[1], **fn[2])
                if inck is not None:
                    ins.then_inc(sems[inck], incv)

        with nc.Block() as block:
            @block.tensor
            def _(e):
                run(e, "pe")

            @block.scalar
            def _(e):
                run(e, "act")

            @block.vector
            def _(e):
                run(e, "dve")

            @block.gpsimd
            def _(e):
                run(e, "pool")

            @block.sync
            def _(e):
                run(e, "sp")


class Buf:
    def __init__(self, t, ntok=1, name=""):
        self.t = t
        self.toks = [Tok(f"{name}{i}") for i in range(ntok)]

    def __getitem__(self, k):
        return self.t[k]


def build_program(NL=DEPTH, final_norm=True, dbg=False):
    nc = bass.Bass("TRN2", target_bir_lowering=False)
    dp = {}

    def din(name, shape, dt=F32):
        dp[name] = nc.dram_tensor(name, list(shape), dt, kind="ExternalInput").ap()
        return dp[name]

    xT_in = din("xT", [128, 16, T])
    xh_in = din("xh", [128, 16, H])
    w_in = din("w_in", [DEPTH, D, NIN])
    w_bp = din("w_br_pool", [DEPTH, 1024, D])
    w_bs = din("w_br_ssm", [DEPTH, D, D])
    w_bc = din("w_br_conv", [DEPTH, 1024, D])
    w_o = din("w_out", [DEPTH, D, D])
    pool_w = din("pool_w", [DEPTH, 4, 256, 256])
    cols_in = din("cols", [128, NCOLS])
    rows_in = din("rows", [128, DEPTH * 96])
    cst_in = din("cst", [128, 4 * 128])
    pc_in = din("pcore", [128, 4 + 96 + 64])
    wsel_in = din("wsel", [4, 3 * 128])
    yT_out = nc.dram_tensor("yT", [128, 16, T], F32, kind="ExternalOutput").ap()
    if dbg:
        dbg_out = nc.dram_tensor("dbg", [128, 8, 16, T], F32, kind="ExternalOutput").ap()
    xspill = nc.dram_tensor("xspill", [128, 16, T], F32).ap()
    pay_h = [nc.dram_tensor(f"pay_h{l}", [128, 256], F32).ap() for l in range(NL)]
    gat_h = [nc.dram_tensor(f"gat_h{l}", [4 * 128, 256], F32).ap() for l in range(NL)]
    pay_s = [[nc.dram_tensor(f"pay_s{l}_{g}", [129, 512], F32).ap() for g in range(4)] for l in range(NL)]
    gat_s = [[nc.dram_tensor(f"gat_s{l}_{g}", [4 * 129, 512], F32).ap() for g in range(4)] for l in range(NL)]

    with contextlib.ExitStack() as es:
        S = Sched(nc, es)
        cur = [16512]

        def alloc(name, shape, dt, at=None, ntok=1):
            esz = 4 if dt == F32 else 2
            n = 1
            for s in shape[1:]:
                n *= s
            nbytes = (n * esz + 31) // 32 * 32
            if at is None:
                off = cur[0]
                cur[0] += nbytes
            else:
                off = at
            t = nc.alloc_sbuf_tensor_at(name, list(shape), dt, offset=off)
            b = Buf(t, ntok, name)
            b.off = off
            b.nbytes = nbytes
            return b

        hT = alloc("hT", [128, 16, TH], BF16, ntok=16)
        W = [alloc(f"W{i}", [128, 16, 512], BF16) for i in range(2)]
        cols = alloc("cols", [128, NCOLS], F32)
        rows = alloc("rows", [128, DEPTH * 96], F32)
        cst = alloc("cst", [128, 4, 128], F32)
        cstb = alloc("cstb", [128, 4, 128], BF16)
        pcore = alloc("pcore", [128, 164], F32)
        wsel = alloc("wsel", [4, 384], F32)
        wdt = alloc("wdt", [128, 16, 32], BF16)
        arow = alloc("arow", [128, 32], F32)
        epst = alloc("epst", [128, 8], F32)
        lat_hl = alloc("lat_hl", [128, 1, 8, 32], BF16)
        lndt = alloc("lndt", [128, 8, 32], F32)
        U0 = cur[0]
        cur[0] += 65536
        U1 = cur[0]
        cur[0] += 32768
        U2 = cur[0]
        cur[0] += 32768
        assert cur[0] <= 229344, cur[0]
        xT = alloc("xTres", [128, 16, T], F32, at=U0, ntok=16)
        xs_tok = alloc("xs_tok", [128, 8, 2048], BF16, at=U0)
        bcT = alloc("bcT", [128, 8, T], BF16, at=U0 + 32768, ntok=8)
        raw = [alloc(f"raw{i}", [128, TH], BF16, at=U0 + 49152 + i * 2080) for i in range(2)]
        acc = [alloc(f"dg{i}", [128, 4, 128], BF16, at=U0 + 49152 + 4160 + i * 1024) for i in range(2)]
        yTg = alloc("yTg", [128, 4, T], F32, at=U0 + 49152, ntok=4)
        pool_out = alloc("pool_out", [128, 8, T], BF16, at=U0, ntok=8)
        conv_out = alloc("conv_out", [128, 8, T], BF16, at=U0 + 16384, ntok=8)
        praw = [alloc(f"praw{i}", [128, TH], F32, at=U0 + 32768 + i * 4160) for i in range(2)]
        pu = alloc("pu", [128, TH], F32, at=U0 + 32768 + 2 * 4160)
        dTt = alloc("dTt", [128, 4, T], BF16, at=U0 + 32768 + 3 * 4160, ntok=4)
        czz = alloc("czz", [128, TH], F32, at=praw[0].off)
        czz.toks = praw[0].toks
        cacc = alloc("cacc", [128, T], F32, at=praw[1].off)
        cacc.toks = praw[1].toks
        assert 32768 + 3 * 4160 + 8192 <= 65536
        gates = alloc("gates", [128, 12, T], BF16, at=U0 + 32768, ntok=12)
        macc = [alloc(f"macc{i}", [128, 512], F32, at=U0 + 32768 + 24576 + i * 2048) for i in range(2)]
        mtmp = [alloc(f"mtmp{i}", [128, 512], F32, at=U0 + 32768 + 24576 + 4096 + i * 2048) for i in range(2)]
        ssm_out = alloc("ssm_out", [128, 16, T], BF16, at=U1, ntok=16)
        sqt = [alloc(f"sqt{i}", [128, TH], BF16, at=U1 + i * 2080) for i in range(2)]
        xhalo = alloc("xhalo", [128, 16, H], F32, at=U1 + 8192)
        ghalo = alloc("ghalo", [128, 4, 256], F32, at=U1 + 12288)
        rstd = alloc("rstd", [128, TH], F32, at=U1 + 16384)
        xsT = [alloc(f"xsT{i}", [128, T], BF16, at=U1 + 28672 + i * 2048) for i in range(2)]
        Fg = alloc("Fg", [128, 512], F32, at=U1 + 26624)
        rstg2 = alloc("rstg2", [128, 512], F32, at=U1 + 24576)
        szt2 = alloc("szt2", [128, 512], BF16, at=U1 + 28672)
        szt2.toks = xsT[0].toks
        u1_users = sqt[0].toks + sqt[1].toks + xhalo.toks + ghalo.toks + rstd.toks + xsT[0].toks + xsT[1].toks + Fg.toks + rstg2.toks
        merged = alloc("merged", [128, 16, T], BF16, at=U2, ntok=16)
        o = [U2]

        def a2(name, shape, dt, ntok=1):
            b = alloc(name, shape, dt, at=o[0], ntok=ntok)
            o[0] += b.nbytes
            return b
        dtt = a2("dtt", [128, 8, 32], F32)
        lat = a2("lat", [128, 8, 32], F32)
        dtte = a2("dtte", [128, 8, 32], F32)
        decb = a2("decb", [128, 8, 32], F32)
        logD = a2("logD", [128, 32], F32)
        coef = a2("coef", [128, 96], F32)
        lD4 = a2("lD4", [4, 32], F32)
        bm_tok = a2("bm_tok", [128, 8, 128], BF16)
        DIg = a2("DIg", [128, 8, 128], BF16)
        Pst = a2("Pst", [128, 512], F32)
        Pb = [a2(f"Pb{i}", [128, 512], BF16) for i in range(2)]
        xdte = [a2(f"xdte{i}", [128, 512], BF16) for i in range(2)]
        rla = a2("rla", [128, 2, 8, 128], BF16)
        CBm2 = [a2(f"CBm{i}", [128, 128], BF16) for i in range(2)]
        decT = a2("decT", [128, 8, 128], BF16)
        Ebc = a2("Ebc", [128, 8, 128], BF16)
        szt = alloc("szt", [128, 512], BF16, at=decT.off)
        szt.toks = decT.toks
        rstg = alloc("rstg", [128, 512], F32, at=Ebc.off)
        rstg.toks = Ebc.toks
        Mp = [a2(f"Mp{i}", [128, 8, 128], BF16) for i in range(2)]
        Cdec = [a2(f"Cdec{i}", [128, 8, 128], BF16) for i in range(2)]
        assert o[0] <= U2 + 32768, (o[0] - U2)
        PS = []
        for i in range(7):
            t = es.enter_context(nc.psum_tensor(f"ps{i}", [128, 512], F32))
            PS.append(Buf(t, 1, f"ps{i}"))
        pst_t = es.enter_context(nc.psum_tensor("pstr", [128, 8, 128], BF16))
        PSTR = Buf(pst_t, 1, "pstr")
        ps_rr = [0]

        def psum():
            b = PS[ps_rr[0] % 5]
            ps_rr[0] += 1
            return b

        sem_w = [S.new_sem("w0"), S.new_sem("w1")]
        sem_x = S.new_sem("xsp")
        sem_xr = [S.new_sem(f"xr{i}") for i in range(16)]
        sem_pay = S.new_sem("pay")
        sem_cc = S.new_sem("cc")
        sem_gin = S.new_sem("gin")
        sem_ld4 = S.new_sem("ld4")
        sem_ccs = [S.new_sem(f"cc{i}") for i in range(6)]
        cc_rr = [0]

        def next_cc_sem():
            k = sem_ccs[cc_rr[0] % 6]
            cc_rr[0] += 1
            return k
        sem_out = S.new_sem("out")
        sem_wdt = S.new_sem("wdt")

        U_f, L_f, ONE_f, I_f = (cst.t[:, i, :] for i in range(4))
        U_b, L_b, ONE_b, I_b = (cstb.t[:, i, :] for i in range(4))
        CST = cst.toks[0]
        CSTB = cstb.toks[0]

        def col(i):
            return cols.t[:, i:i + 1]

        S.dma("sp", lambda e: e.dma_start(out=cols.t[:], in_=cols_in), S.new_sem("l1"), writes=cols.toks)
        S.dma("sp", lambda e: e.dma_start(out=rows.t[:], in_=rows_in), S.new_sem("l2"), writes=rows.toks)
        S.dma("sp", lambda e: e.dma_start(out=cst.t[:], in_=cst_in.rearrange("p (a b) -> p a b", a=4)), S.new_sem("l3"), writes=cst.toks)
        S.dma("sp", lambda e: e.dma_start(out=pcore.t[:], in_=pc_in), S.new_sem("l4"), writes=pcore.toks)
        S.dma("sp", lambda e: e.dma_start(out=wsel.t[:], in_=wsel_in), S.new_sem("l5"), writes=wsel.toks)
        for n in range(16):
            S.dma("sp", lambda e: e.dma_start(out=xT.t[:, n, :], in_=xT_in[:, n, :]), sem_xr[n], writes=[xT.toks[n]])
        S.dma("sp", lambda e: e.dma_start(out=xhalo.t[:], in_=xh_in), S.new_sem("l6"), writes=xhalo.toks)
        S.op("dve", lambda e: e.tensor_copy(out=cstb.t[:], in_=cst.t[:]), reads=cst.toks, writes=cstb.toks)
        S.op("dve", lambda e: e.memset(epst.t[:], EPS), writes=epst.toks)
        epsc = epst.t[:, 0:1]
        selh = pcore.t[:, 0:4]
        negmask = pcore.t[:, 4:100]
        poolcorr = pcore.t[:, 100:164]

        wstate = {"n": 0}

        def wload(src_aps, kcs):
            b = wstate["n"] % 2
            wstate["n"] += 1
            buf = W[b]
            for i, (src, c0) in enumerate(src_aps):
                ncols = src.shape[1]
                v = src.rearrange("(kc p) n -> p kc n", p=128)
                S.dma("pool", lambda e, v=v, c0=c0, ncols=ncols, buf=buf, kcs=kcs:
                      e.dma_start(out=buf.t[:, 0:kcs, c0:c0 + ncols], in_=v),
                      sem_w[b], writes=buf.toks)
            return buf

        def mm_group(ps, wbuf, m0, kcs, rhs_fn, rhs_toks, ncol, out_cols=None):
            for kc in range(kcs):
                S.op("pe", lambda e, kc=kc: e.matmul(
                    ps.t[:, 0:ncol] if out_cols is None else ps.t[:, out_cols[0]:out_cols[1]],
                    wbuf.t[:, kc, m0:m0 + 128], rhs_fn(kc), start=(kc == 0), stop=(kc == kcs - 1)),
                    reads=wbuf.toks + ([rhs_toks[kc]] if len(rhs_toks) == kcs else rhs_toks), writes=ps.toks, inc=(kc == kcs - 1))

        pe_pending = []

        def flush_pe_pending():
            while pe_pending:
                pe_pending.pop(0)()

        def inproj_chunk(wbuf, m0, halo, consume, flush_at=0):
            for half in range(2):
                ps = psum()
                mm_group(ps, wbuf, m0, 16, lambda kc, half=half: hT.t[:, kc, H + half * 512:H + (half + 1) * 512], hT.toks, 512)
                if half >= flush_at:
                    flush_pe_pending()
                consume(half, ps)
            if halo:
                ps = psum()
                mm_group(ps, wbuf, m0, 16, lambda kc: hT.t[:, kc, 0:H], hT.toks, H)
                consume(2, ps)

        for l in range(NL + (1 if final_norm else 0)):
            is_final = (l == NL)
            cb = l * NCOL_L if not is_final else DEPTH * NCOL_L
            psn = [PS[5], PS[6], psum()]
            if l == 0:
                for kc in range(16):
                    sq = sqt[kc % 2]
                    S.op("act", lambda e: e.activation(out=sq.t[:, H:TH], in_=xT.t[:, kc, :], func=AF.Square),
                         reads=[xT.toks[kc]], writes=sq.toks)
                    for part in range(2):
                        lo, hi = (H, H + 512) if part == 0 else (H + 512, TH)
                        S.op("pe", lambda e: e.matmul(
                            psn[part].t[:, 0:hi - lo], ONE_b, sq.t[:, lo:hi], start=(kc == 0), stop=(kc == 15)),
                            reads=sq.toks + [CSTB], writes=psn[part].toks)
            for part in range(2):
                lo, hi = (H, H + 512) if part == 0 else (H + 512, TH)
                S.op("act", lambda e: e.activation(out=rstd.t[:, lo:hi], in_=psn[part].t[:, 0:hi - lo], func=AF.Ln, bias=epsc, scale=1.0 / D),
                     reads=psn[part].toks + epst.toks, writes=rstd.toks)
                S.op("act", lambda e: e.activation(out=rstd.t[:, lo:hi], in_=rstd.t[:, lo:hi], func=AF.Exp, scale=-0.5), reads=rstd.toks, writes=rstd.toks)
            if is_final:
                for kc in range(16):
                    S.op("dve", lambda e: e.scalar_tensor_tensor(
                        out=xT.t[:, kc, :], in0=xT.t[:, kc, :], scalar=col(cb + kc), in1=rstd.t[:, H:TH],
                        op0=ALU.mult, op1=ALU.mult), reads=[xT.toks[kc]] + rstd.toks + cols.toks, writes=[xT.toks[kc]])
                    S.dma("sp", lambda e: e.dma_start(out=yT_out[:, kc, :], in_=xT.t[:, kc, :]), sem_out, reads=[xT.toks[kc]])
                break
            for kc in range(16):
                S.op("dve", lambda e: e.scalar_tensor_tensor(
                    out=hT.t[:, kc, H:TH], in0=xT.t[:, kc, :], scalar=col(cb + C_NW + kc), in1=rstd.t[:, H:TH],
                    op0=ALU.mult, op1=ALU.mult), reads=[xT.toks[kc]] + rstd.toks + cols.toks, writes=[hT.toks[kc]])
            u0_users = (xs_tok.toks + bcT.toks + raw[0].toks + raw[1].toks + acc[0].toks + acc[1].toks + yTg.toks
                        + pool_out.toks + conv_out.toks + praw[0].toks + praw[1].toks + pu.toks + dTt.toks + czz.toks
                        + cacc.toks + gates.toks + macc[0].toks + macc[1].toks + mtmp[0].toks + mtmp[1].toks)
            S.dma("sp", lambda e: e.dma_start(out=xspill, in_=xT.t[:]), sem_x, reads=xT.toks, writes=u0_users)
            u2_users = (dtt.toks + lat.toks + dtte.toks + decb.toks + logD.toks + coef.toks + lD4.toks + bm_tok.toks + DIg.toks
                        + Pst.toks + Pb[0].toks + Pb[1].toks + xdte[0].toks + xdte[1].toks + rla.toks + CBm2[0].toks + CBm2[1].toks
                        + decT.toks + Ebc.toks + Mp[0].toks + Mp[1].toks + Cdec[0].toks + Cdec[1].toks)
            if l > 0:
                xh2 = xhalo.t[:].rearrange("p k h -> p (k h)")
                S.op("dve", lambda e: e.tensor_scalar(out=xh2, in0=ghalo.t[:, 0, :], scalar1=selh[:, 0:1], scalar2=None, op0=ALU.mult),
                     reads=ghalo.toks + pcore.toks, writes=xhalo.toks)
                for jx in range(1, 4):
                    S.op("dve", lambda e: e.scalar_tensor_tensor(out=xh2, in0=ghalo.t[:, jx, :], scalar=selh[:, jx:jx + 1], in1=xh2, op0=ALU.mult, op1=ALU.add),
                         reads=ghalo.toks + pcore.toks + xhalo.toks, writes=xhalo.toks)
            sqh = sqt[0]
            S.op("act", lambda e: e.activation(out=sqh.t[:, 0:256].rearrange("p (k h) -> p k h", k=16), in_=xhalo.t[:], func=AF.Square),
                 reads=xhalo.toks, writes=sqh.toks)
            for kc in range(16):
                S.op("pe", lambda e: e.matmul(psn[2].t[:, 0:H], ONE_b, sqh.t[:, kc * H:(kc + 1) * H], start=(kc == 0), stop=(kc == 15)),
                     reads=sqh.toks + [CSTB], writes=psn[2].toks, inc=(kc == 15))
            S.op("act", lambda e: e.activation(out=rstd.t[:, 0:H], in_=psn[2].t[:, 0:H], func=AF.Ln, bias=epsc, scale=1.0 / D),
                 reads=psn[2].toks + epst.toks, writes=rstd.toks)
            S.op("act", lambda e: e.activation(out=rstd.t[:, 0:H], in_=rstd.t[:, 0:H], func=AF.Exp, scale=-0.5), reads=rstd.toks, writes=rstd.toks)
            for kc in range(16):
                S.op("dve", lambda e: e.scalar_tensor_tensor(
                    out=hT.t[:, kc, 0:H], in0=xhalo.t[:, kc, :], scalar=col(cb + C_NW + kc), in1=rstd.t[:, 0:H],
                    op0=ALU.mult, op1=ALU.mult), reads=xhalo.toks + rstd.toks + cols.toks, writes=[hT.toks[kc]])
            wl = w_in[l]
            S.dma("pool", lambda e, wl=wl: e.dma_start(out=wdt.t[:], in_=wl[:, OFF_DT:OFF_DT + 32].rearrange("(kc p) n -> p kc n", p=128)),
                  sem_wdt, writes=wdt.toks)
            r0 = l * 96
            dtb_row = rows.t[:, r0:r0 + 32]
            alog_row = rows.t[:, r0 + 32:r0 + 64]
            d_row = rows.t[:, r0 + 64:r0 + 96]
            S.op("act", lambda e, alog_row=alog_row: e.activation(out=arow.t[:], in_=alog_row, func=AF.Exp), reads=rows.toks, writes=arow.toks)
            S.op("dve", lambda e: e.tensor_scalar(out=arow.t[:], in0=arow.t[:], scalar1=-1.0, scalar2=None, op0=ALU.mult),
                 reads=arow.toks, writes=arow.toks)

            pend = {"B": None, "C": None}

            def xbc_stageA(wbuf, m0, ch, out_ap, out_toks, ri, post):
                rw = raw[ri % 2]
                dg = acc[ri % 2]

                def consume(part, ps):
                    lo, hi = (H, H + 512) if part == 0 else ((H + 512, TH) if part == 1 else (0, H))
                    S.op("act", lambda e: e.activation(out=rw.t[:, lo:hi], in_=ps.t[:, 0:hi - lo], func=AF.Copy),
                         reads=ps.toks, writes=rw.toks)
                inproj_chunk(wbuf, m0, True, consume)
                wc = cb + C_SCW + ch
                S.op("dve", lambda e: e.tensor_tensor(
                    out=dg.t[:], in0=I_f.unsqueeze(1).broadcast_to([128, 4, 128]),
                    in1=cols.t[:, wc:wc + 73:24].unsqueeze(2).broadcast_to([128, 4, 128]), op=ALU.mult),
                    reads=[CST] + cols.toks, writes=dg.toks)

                def stageB():
                    for half in range(2):
                        ps = psum()
                        for k in range(4):
                            sh = 3 - k
                            lo = H - sh + half * 512
                            S.op("pe", lambda e: e.matmul(ps.t[:, 0:512], dg.t[:, k, :], rw.t[:, lo:lo + 512], start=(k == 0), stop=(k == 3)),
                                 reads=dg.toks + rw.toks, writes=ps.toks, inc=(k == 3))
                        S.op("act", lambda e: e.activation(out=out_ap[:, half * 512:(half + 1) * 512], in_=ps.t[:, 0:512], func=AF.Silu,
                                                           bias=col(cb + C_SCB + ch), scale=1.0),
                             reads=ps.toks + cols.toks, writes=out_toks)
                    return post
                return stageB

            def xbc_step(stageB_new):
                c_new = pend["B"]() if pend["B"] is not None else None
                if pend["C"] is not None:
                    pend["C"]()
                pend["B"] = stageB_new
                pend["C"] = c_new

            def xbc_flush():
                while pend["B"] is not None or pend["C"] is not None:
                    xbc_step(None)

            ri = 0
            for blk in range(2):
                wb = wload([(wl[:, OFF_B + blk * 512:OFF_B + (blk + 1) * 512], 0)], 16)
                for mm in range(4):
                    j = blk * 4 + mm
                    xbc_step(xbc_stageA(wb, mm * 128, 16 + j, bcT.t[:, j, :], [bcT.toks[j]], ri, None))
                    ri += 1
            psd = psum()
            for c in range(8):
                for kc in range(16):
                    S.op("pe", lambda e, c=c, kc=kc: e.matmul(
                        psd.t[:, c * 32:(c + 1) * 32], hT.t[:, kc, H + c * 128:H + (c + 1) * 128], wdt.t[:, kc, :],
                        start=(kc == 0), stop=(kc == 15)), reads=hT.toks + wdt.toks, writes=psd.toks, inc=(kc == 15))
            dtt3 = dtt.t[:]
            S.op("dve", lambda e: e.tensor_tensor(out=dtt.t[:], in0=psd.t[:, 0:256].rearrange("p (c h) -> p c h", c=8),
                                                  in1=dtb_row.unsqueeze(1).broadcast_to([128, 8, 32]), op=ALU.add),
                 reads=psd.toks + rows.toks, writes=dtt.toks)
            S.op("act", lambda e: e.activation(out=dtt.t[:], in_=dtt.t[:], func=AF.Exp), reads=dtt.toks, writes=dtt.toks)
            S.op("act", lambda e: e.activation(out=dtt.t[:], in_=dtt.t[:], func=AF.Ln, bias=1.0, scale=1.0), reads=dtt.toks, writes=dtt.toks)
            S.op("dve", lambda e: e.tensor_tensor(out=lat.t[:], in0=dtt.t[:], in1=arow.t[:].unsqueeze(1).broadcast_to([128, 8, 32]), op=ALU.mult),
                 reads=dtt.toks + arow.toks, writes=lat.toks)
            S.op("dve", lambda e: e.tensor_copy(out=lat_hl.t[:, 0], in_=lat.t[:]), reads=lat.toks, writes=lat_hl.toks)
            S.op("act", lambda e: e.activation(out=lndt.t[:], in_=dtt.t[:], func=AF.Ln), reads=dtt.toks, writes=lndt.toks)
            lat2 = lat.t[:].rearrange("p c h -> p (c h)")
            ps_te = psum()
            S.op("pe", lambda e: e.matmul(ps_te.t[:, 0:256], L_f, lat2, start=True, stop=True), reads=lat.toks + [CST], writes=ps_te.toks)
            ps_tot = psum()
            S.op("pe", lambda e: e.matmul(ps_tot.t[:, 0:256], ONE_f, lat2, start=True, stop=True), reads=lat.toks + [CST], writes=ps_tot.toks)
            ps_ld = psum()
            for c in range(8):
                S.op("pe", lambda e, c=c: e.matmul(ps_ld.t[:, 0:32], ONE_f, lat.t[:, c, :], start=(c == 0), stop=(c == 7)),
                     reads=lat.toks + [CST], writes=ps_ld.toks, inc=(c == 7))
            S.op("act", lambda e: e.activation(out=dtte.t[:].rearrange("p c h -> p (c h)"), in_=ps_te.t[:, 0:256], func=AF.Exp),
                 reads=ps_te.toks, writes=dtte.toks)
            S.op("dve", lambda e: e.tensor_tensor(out=dtte.t[:], in0=dtte.t[:], in1=dtt.t[:], op=ALU.mult), reads=dtte.toks + dtt.toks, writes=dtte.toks)
            S.op("act", lambda e: e.activation(out=decb.t[:].rearrange("p c h -> p (c h)"), in_=ps_tot.t[:, 0:256], func=AF.Exp),
                 reads=ps_tot.toks, writes=decb.toks)
            S.op("act", lambda e: e.activation(out=logD.t[:], in_=ps_ld.t[:, 0:32], func=AF.Copy), reads=ps_ld.toks, writes=logD.toks)

            def bc8(ap2, g):
                return ap2[:, g * 8:(g + 1) * 8].unsqueeze(2).broadcast_to([128, 8, 64])

            def su_x(g, c):
                xd = xdte[c % 2]
                S.op(SU_ENG, lambda e: e.tensor_tensor(
                    out=xd.t[:].rearrange("p (h q) -> p h q", h=8),
                    in0=xs_tok.t[:, c, g * 512:(g + 1) * 512].rearrange("p (h q) -> p h q", h=8),
                    in1=bc8(dtte.t[:, c, :], g), op=ALU.mult), reads=xs_tok.toks + dtte.toks, writes=xd.toks)

            def su_m(g, c, Pt, first):
                xd = xdte[c % 2]
                pss = psum()
                S.op("pe", lambda e: e.matmul(pss.t[:, 0:512], bm_tok.t[:, c, :], xd.t[:], start=True, stop=True),
                     reads=bm_tok.toks + xd.toks, writes=pss.toks)
                if first:
                    S.op("act", lambda e: e.activation(out=Pt.t[:], in_=pss.t[:, 0:512], func=AF.Copy), reads=pss.toks, writes=Pt.toks)
                else:
                    S.op(SU_ENG, lambda e: e.tensor_tensor(
                        out=Pt.t[:].rearrange("p (h q) -> p h q", h=8), in0=Pt.t[:].rearrange("p (h q) -> p h q", h=8),
                        in1=bc8(decb.t[:, c, :], g), op=ALU.mult), reads=Pt.toks + decb.toks, writes=Pt.toks)
                    S.op("dve", lambda e: e.tensor_tensor(out=Pt.t[:], in0=pss.t[:, 0:512], in1=Pt.t[:], op=ALU.add),
                         reads=Pt.toks + pss.toks, writes=Pt.toks)

            def state_update(g, c, Pt, first):
                su_x(g, c)
                su_m(g, c, Pt, first)

            def defer_state_chain(g):
                deferred.append(lambda: su_x(g, 0))
                for c in range(8):
                    if c < 7:
                        deferred.append(lambda c=c: su_x(g, c + 1))
                    deferred.append(lambda c=c: su_m(g, c, Pst, c == 0))

            deferred = []

            def run_deferred(n):
                for _ in range(min(n, len(deferred))):
                    deferred.pop(0)()

            def mk_xs_transposes(j, xt_):
                def f():
                    for c in range(8):
                        S.op("pe", lambda e: e.transpose(PSTR.t[:, c, :], xt_.t[:, c * 128:(c + 1) * 128], I_b),
                             reads=xt_.toks + [CSTB], writes=PSTR.toks, inc=(c == 7))
                    S.op("act", lambda e: e.activation(out=xs_tok.t[:, :, j * 128:(j + 1) * 128], in_=PSTR.t[:], func=AF.Copy),
                         reads=PSTR.toks, writes=xs_tok.toks)
                return f

            def mk_bm_transposes(g):
                def f():
                    for c in range(8):
                        S.op("pe", lambda e: e.transpose(PSTR.t[:, c, :], bcT.t[:, g, c * 128:(c + 1) * 128], I_b),
                             reads=[bcT.toks[g], CSTB], writes=PSTR.toks, inc=(c == 7))
                    S.op("act", lambda e: e.activation(out=bm_tok.t[:], in_=PSTR.t[:], func=AF.Copy), reads=PSTR.toks, writes=bm_tok.toks)
                return f

            cc_toks = [Tok(f"cc{g}") for g in range(4)]

            def mk_exchange(g):
                def f():
                    S.dma("sp", lambda e: e.dma_start(out=pay_s[l][g][0:128, :], in_=Pst.t[:]), sem_pay, reads=Pst.toks)
                    S.dma("sp", lambda e: e.dma_start(out=pay_s[l][g][128:129, :], in_=Pst.t[0:1, :]), sem_pay, reads=Pst.toks)
                    S.ops["sp"].append(([(sem_pay, S.count[sem_pay])], None, None, 0))
                    S.dma("sp", lambda e: e.dma_start(out=pay_s[l][g][128:129, 0:32], in_=logD.t[0:1, :]), sem_pay, reads=logD.toks)
                    pay_wait = (sem_pay, S.count[sem_pay])
                    S.ops["pool"].append(([pay_wait], None, None, 0))
                    S.dma("pool", lambda e: e.collective_compute(
                        "AllGather", ALU.bypass, replica_groups=[[0, 1, 2, 3], [4, 5, 6, 7]],
                        ins=[pay_s[l][g]], outs=[gat_s[l][g]]), next_cc_sem(), writes=[cc_toks[g]], inc=1)
                return f

            for g in range(4):
                wb = wload([(wl[:, OFF_XS + g * 512:OFF_XS + (g + 1) * 512], 0)], 16)
                for mm in range(4):
                    j = g * 4 + mm
                    xt_ = xsT[j % 2]
                    xbc_step(xbc_stageA(wb, mm * 128, j, xt_.t[:], xt_.toks, ri, mk_xs_transposes(j, xt_)))
                    ri += 1
                    run_deferred(5)
                    if mm == 1 and g > 0:
                        deferred.append(mk_bm_transposes(g - 1))
                        defer_state_chain(g - 1)
                        deferred.append(mk_exchange(g - 1))
            xbc_flush()
            deferred.append(mk_bm_transposes(3))
            defer_state_chain(3)
            deferred.append(mk_exchange(3))
            run_deferred(len(deferred))

            rlat = [Tok("rla0"), Tok("rla1")]
            rla.toks.extend(rlat)

            def prep1a(g, c, b):
                tsl = slice(c * 128, (c + 1) * 128)
                pcb = psum()
                S.op("pe", lambda e: e.matmul(pcb.t[:, 0:128], bcT.t[:, g, tsl], bcT.t[:, 4 + g, tsl], start=True, stop=True),
                     reads=[bcT.toks[g], bcT.toks[4 + g]], writes=pcb.toks)
                S.op("dve", lambda e: e.tensor_tensor(
                    out=rla.t[:, b], in0=U_b.unsqueeze(1).broadcast_to([128, 8, 128]),
                    in1=lat_hl.t[:, 0, c, g * 8:(g + 1) * 8].unsqueeze(2).broadcast_to([128, 8, 128]), op=ALU.mult),
                    reads=[CSTB] + lat_hl.toks, writes=[rlat[b]] + ([rla.toks[0]] if b == 0 else []))
                S.op("dve", lambda e: e.tensor_tensor(out=CBm2[b].t[:], in0=pcb.t[:, 0:128], in1=U_f, op=ALU.mult),
                     reads=pcb.toks + [CST], writes=CBm2[b].toks)

            def prep1b(g, c, b):
                pq = {}
                for (nm, lhs) in (("seg", L_b), ("abc", ONE_b)):
                    for hh in range(2):
                        pseg = psum()
                        rsl = rla.t[:, b, hh * 4:(hh + 1) * 4, :].rearrange("p h l -> p (h l)")
                        S.op("pe", lambda e: e.matmul(pseg.t[:, 0:512], lhs, rsl, start=True, stop=True),
                             reads=[rlat[b], CSTB], writes=pseg.toks)
                        pq[(nm, hh)] = pseg
                for hl in range(8):
                    h = g * 8 + hl
                    ps_ = pq[("seg", hl // 4)]
                    S.op("act", lambda e: e.activation(out=decT.t[:, hl, :], in_=ps_.t[:, (hl % 4) * 128:(hl % 4 + 1) * 128], func=AF.Exp,
                                                       bias=lndt.t[:, c, h:h + 1], scale=1.0),
                         reads=ps_.toks + lndt.toks, writes=decT.toks)
                for hh in range(2):
                    ps_ = pq[("abc", hh)]
                    S.op("act", lambda e: e.activation(out=Ebc.t[:, hh * 4:(hh + 1) * 4, :].rearrange("p h l -> p (h l)"), in_=ps_.t[:, 0:512], func=AF.Exp),
                         reads=ps_.toks, writes=Ebc.toks)

            def prep2(g, c, b):
                tsl = slice(c * 128, (c + 1) * 128)
                S.op("dve", lambda e: e.tensor_tensor(out=Mp[b].t[:], in0=decT.t[:], in1=CBm2[b].t[:].unsqueeze(1).broadcast_to([128, 8, 128]), op=ALU.mult),
                     reads=decT.toks + CBm2[b].toks, writes=Mp[b].toks)
                S.op("dve", lambda e: e.tensor_tensor(
                    out=Cdec[b].t[:], in0=Ebc.t[:], in1=bcT.t[:, 4 + g, tsl].unsqueeze(1).broadcast_to([128, 8, 128]), op=ALU.mult),
                    reads=Ebc.toks + [bcT.toks[4 + g]], writes=Cdec[b].toks)

            def early_start(g):
                S.op("dve", lambda e: e.tensor_tensor(
                    out=DIg.t[:], in0=I_f.unsqueeze(1).broadcast_to([128, 8, 128]),
                    in1=d_row[:, g * 8:(g + 1) * 8].unsqueeze(2).broadcast_to([128, 8, 128]), op=ALU.mult),
                    reads=[CST] + rows.toks, writes=DIg.toks)
                mk_bm_transposes(g)()
                prep1a(g, 0, 0)
                prep1a(g, 1, 1)
                prep1b(g, 0, 0)

            for g in range(4):
                gv = gat_s[l][g].rearrange("(r q) f -> q r f", r=4)
                if g == 0:
                    early_start(0)
                szt_g, rstg_g = (szt2, rstg2) if g < 3 else (szt, rstg)
                if g == 0:
                    S.dma("sp", lambda e: e.dma_start(out=lD4.t[:], in_=gv[128, :, 0:32]), sem_ld4, reads=[cc_toks[g]], writes=lD4.toks)
                    psc_ = psum()
                    for jx in range(3):
                        S.op("pe", lambda e: e.matmul(psc_.t[:, jx * 32:(jx + 1) * 32], wsel.t[:, jx * 128:(jx + 1) * 128], lD4.t[:],
                                                      start=True, stop=True), reads=wsel.toks + lD4.toks, writes=psc_.toks)
                    S.op("dve", lambda e: e.tensor_tensor(out=coef.t[:], in0=psc_.t[:, 0:96], in1=negmask, op=ALU.add),
                         reads=psc_.toks + pcore.toks, writes=coef.toks)
                    S.op("act", lambda e: e.activation(out=coef.t[:], in_=coef.t[:], func=AF.Exp), reads=coef.toks, writes=coef.toks)
                for jx in range(3):
                    S.dma("sp", lambda e: e.dma_start(out=Fg.t[:], in_=gv[0:128, jx, :]), sem_gin, reads=[cc_toks[g]], writes=Fg.toks)
                    cf = coef.t[:, jx * 32 + g * 8: jx * 32 + (g + 1) * 8].unsqueeze(2).broadcast_to([128, 8, 64])
                    fj = Fg.t[:].rearrange("p (h q) -> p h q", h=8)
                    if jx == 0:
                        S.op("dve", lambda e: e.tensor_tensor(out=Pst.t[:].rearrange("p (h q) -> p h q", h=8), in0=fj, in1=cf, op=ALU.mult),
                             reads=Fg.toks + coef.toks, writes=Pst.toks)
                    else:
                        S.op("dve", lambda e: e.tensor_tensor(out=fj, in0=fj, in1=cf, op=ALU.mult),
                             reads=Fg.toks + coef.toks, writes=Fg.toks)
                        S.op("dve", lambda e: e.tensor_tensor(out=Pst.t[:], in0=Pst.t[:], in1=Fg.t[:], op=ALU.add),
                             reads=Fg.toks + Pst.toks, writes=Pst.toks)
                S.op("act", lambda e: e.activation(out=Pb[0].t[:], in_=Pst.t[:], func=AF.Copy), reads=Pst.toks, writes=Pb[0].toks)
                prep2(g, 0, 0)
                for c in range(8):
                    pbc = Pb[c % 2]
                    b = c % 2
                    tsl = slice(c * 128, (c + 1) * 128)
                    if c < 7:
                        prep1b(g, c + 1, (c + 1) % 2)
                        if c < 6:
                            prep1a(g, c + 2, c % 2)
                        state_update(g, c, Pst, False)
                        S.op("act", lambda e: e.activation(out=Pb[(c + 1) % 2].t[:], in_=Pst.t[:], func=AF.Copy),
                             reads=Pst.toks, writes=Pb[(c + 1) % 2].toks)
                        prep2(g, c + 1, (c + 1) % 2)
                    py = psum()
                    for hl in range(8):
                        jj, hh = hl // 2, hl % 2
                        ch0 = (g * 8 + hl) * 64
                        outp = py.t[hh * 64:(hh + 1) * 64, jj * 128:(jj + 1) * 128]
                        tp = (0, hh * 64)
                        S.op("pe", lambda e: e.matmul(
                            outp, xs_tok.t[:, c, ch0:ch0 + 64], Mp[b].t[:, hl, :], start=True, stop=False, tile_position=tp),
                            reads=xs_tok.toks + Mp[b].toks, writes=py.toks, inc=False)
                        S.op("pe", lambda e: e.matmul(
                            outp, xs_tok.t[:, c, ch0:ch0 + 64], DIg.t[:, hl, :], start=False, stop=False, tile_position=tp),
                            reads=xs_tok.toks + DIg.toks, writes=py.toks, inc=False)
                        S.op("pe", lambda e: e.matmul(
                            outp, pbc.t[:, hl * 64:(hl + 1) * 64], Cdec[b].t[:, hl, :], start=False, stop=True, tile_position=tp),
                            reads=pbc.toks + Cdec[b].toks, writes=py.toks, inc=(hl == 7))
                    S.op("dve", lambda e: e.tensor_copy(
                        out=yTg.t[:, :, tsl], in_=py.t[:, 0:512].rearrange("p (j l) -> p j l", j=4)),
                        reads=py.toks, writes=yTg.toks)
                wb = wload([(wl[:, OFF_Z + g * 512:OFF_Z + (g + 1) * 512], 0)], 16)
                psg = [PS[5], PS[6]]
                for mm in range(4):
                    j = g * 4 + mm

                    def consume(part, ps, mm=mm):
                        sl = slice(part * 512, (part + 1) * 512)
                        S.op("act", lambda e: e.activation(out=szt_g.t[:], in_=ps.t[:, 0:512], func=AF.Silu), reads=ps.toks, writes=szt_g.toks)
                        S.op("dve", lambda e: e.tensor_tensor(out=yTg.t[:, mm, sl], in0=yTg.t[:, mm, sl], in1=szt_g.t[:], op=ALU.mult),
                             reads=szt_g.toks + yTg.toks, writes=yTg.toks)
                        S.op("act", lambda e: e.activation(out=szt_g.t[:], in_=yTg.t[:, mm, sl], func=AF.Square), reads=yTg.toks + szt_g.toks, writes=szt_g.toks)
                        pe_pending.append(lambda: S.op("pe", lambda e: e.matmul(psg[part].t[:, 0:512], ONE_b, szt_g.t[:], start=(mm == 0), stop=(mm == 3)),
                                                       reads=szt_g.toks + [CSTB], writes=psg[part].toks))
                    inproj_chunk(wb, mm * 128, False, consume)
                flush_pe_pending()
                if g < 3:
                    early_start(g + 1)
                for part in range(2):
                    sl = slice(part * 512, (part + 1) * 512)
                    S.op("act", lambda e: e.activation(out=rstg_g.t[:], in_=psg[part].t[:, 0:512], func=AF.Ln, bias=epsc, scale=1.0 / 512),
                         reads=psg[part].toks + epst.toks, writes=rstg_g.toks)
                    S.op("act", lambda e: e.activation(out=rstg_g.t[:], in_=rstg_g.t[:], func=AF.Exp, scale=-0.5), reads=rstg_g.toks, writes=rstg_g.toks)
                    for mm in range(4):
                        j = g * 4 + mm
                        S.op("dve", lambda e: e.scalar_tensor_tensor(
                            out=ssm_out.t[:, j, sl], in0=yTg.t[:, mm, sl], scalar=col(cb + C_SNW + j), in1=rstg_g.t[:], op0=ALU.mult, op1=ALU.mult),
                            reads=yTg.toks + rstg_g.toks + cols.toks, writes=[ssm_out.toks[j]] + u1_users)

            for blk in range(2):
                wb = wload([(wl[:, OFF_PG + blk * 512:OFF_PG + (blk + 1) * 512], 0)], 16)
                for mm in range(4):
                    j = blk * 4 + mm

                    def consume(part, ps, j=j):
                        sl = slice(part * 512, (part + 1) * 512)
                        S.op("act", lambda e: e.activation(out=pool_out.t[:, j, sl], in_=ps.t[:, 0:512], func=AF.Silu),
                             reads=ps.toks, writes=[pool_out.toks[j]])
                    inproj_chunk(wb, mm * 128, False, consume)
            pwv = pool_w[l].rearrange("g (kc p) d -> p (g kc) d", p=128)
            for blk in range(2):
                wb = wload([(wl[:, OFF_PU + blk * 512:OFF_PU + (blk + 1) * 512], 0)], 16)
                for mm in range(4):
                    j = blk * 4 + mm
                    gp = j // 2
                    win = POOL_WINDOWS[gp]

                    def consume(part, ps):
                        lo, hi = (H, H + 512) if part == 0 else ((H + 512, TH) if part == 1 else (0, H))
                        S.op("act", lambda e: e.activation(out=pu.t[:, lo:hi], in_=ps.t[:, 0:hi - lo], func=AF.Copy), reads=ps.toks, writes=pu.toks)
                    inproj_chunk(wb, mm * 128, True, consume, flush_at=1)
                    src = pu
                    sh = 1
                    k = 0
                    while sh < win:
                        dst = praw[k % 2]
                        lo = 2 * sh - 1
                        S.op("dve", lambda e, src=src, dst=dst, sh=sh, lo=lo: e.tensor_tensor(
                            out=dst.t[:, lo:TH], in0=src.t[:, lo:TH], in1=src.t[:, lo - sh:TH - sh], op=ALU.add),
                            reads=src.toks, writes=dst.toks)
                        src = dst
                        sh *= 2
                        k += 1
                    S.op("dve", lambda e, src=src, gp=gp: e.tensor_tensor(out=src.t[:, H:2 * H], in0=src.t[:, H:2 * H], in1=poolcorr[:, gp * 16:(gp + 1) * 16], op=ALU.mult),
                         reads=src.toks + pcore.toks, writes=src.toks)
                    S.op("dve", lambda e, src=src, win=win, j=j: e.scalar_tensor_tensor(
                        out=dTt.t[:, j % 4, :], in0=src.t[:, H:TH], scalar=1.0 / win, in1=pu.t[:, H:TH], op0=ALU.mult, op1=ALU.subtract),
                        reads=src.toks + pu.toks, writes=[dTt.toks[j % 4]])
                    if j % 2 == 1:
                        if j == 1:
                            wpw = wload([(pool_w[l].rearrange("g k d -> (g k) d")[0:1024, :], 0)], 8)

                        def mk_group(gp=gp, wpw=wpw):
                            def f():
                                d0 = (2 * gp) % 4
                                for dd in range(2):
                                    jo = gp * 2 + dd
                                    for part in range(2):
                                        ps = psum()
                                        sl = slice(part * 512, (part + 1) * 512)
                                        for k2 in range(2):
                                            S.op("pe", lambda e: e.matmul(
                                                ps.t[:, 0:512], wpw.t[:, gp * 2 + k2, dd * 128:(dd + 1) * 128], dTt.t[:, d0 + k2, sl],
                                                start=(k2 == 0), stop=(k2 == 1)), reads=wpw.toks + [dTt.toks[d0], dTt.toks[d0 + 1]], writes=ps.toks, inc=(k2 == 1))
                                        S.op("dve", lambda e: e.scalar_tensor_tensor(
                                            out=pool_out.t[:, jo, sl], in0=ps.t[:, 0:512], scalar=col(cb + C_PSC + jo), in1=pool_out.t[:, jo, sl],
                                            op0=ALU.mult, op1=ALU.mult), reads=ps.toks + [pool_out.toks[jo]] + cols.toks, writes=[pool_out.toks[jo]])
                            return f
                        pe_pending.append(mk_group())
            flush_pe_pending()

            for (off, kind) in ((OFF_CG, "g"), (OFF_CB, "b")):
                for blk in range(2):
                    wb = wload([(wl[:, off + blk * 512:off + (blk + 1) * 512], 0)], 16)
                    for mm in range(4):
                        j = blk * 4 + mm

                        def consume(part, ps, j=j, kind=kind):
                            sl = slice(part * 512, (part + 1) * 512)
                            if kind == "g":
                                S.op("act", lambda e: e.activation(out=conv_out.t[:, j, sl], in_=ps.t[:, 0:512], func=AF.Silu),
                                     reads=ps.toks, writes=[conv_out.toks[j]])
                            else:
                                S.op("dve", lambda e: e.tensor_tensor(out=conv_out.t[:, j, sl], in0=ps.t[:, 0:512], in1=conv_out.t[:, j, sl], op=ALU.mult),
                                     reads=ps.toks + [conv_out.toks[j]], writes=[conv_out.toks[j]])
                        inproj_chunk(wb, mm * 128, False, consume)
            for pr in range(4):
                j0 = pr * 2
                wb = wload([(wl[:, OFF_CC + j0 * 128:OFF_CC + (j0 + 2) * 128], 0), (wl[:, OFF_CV + j0 * 128:OFF_CV + (j0 + 2) * 128], 256)], 16)
                for mm in range(2):
                    j = j0 + mm

                    def consume_c(part, ps):
                        lo, hi = (H, H + 512) if part == 0 else ((H + 512, TH) if part == 1 else (0, H))
                        S.op("act", lambda e: e.activation(out=pu.t[:, lo:hi], in_=ps.t[:, 0:hi - lo], func=AF.Copy), reads=ps.toks, writes=pu.toks)
                    inproj_chunk(wb, mm * 128, True, consume_c)

                    def consume_v(part, ps):
                        lo, hi = (H, H + 512) if part == 0 else ((H + 512, TH) if part == 1 else (0, H))
                        S.op("dve", lambda e: e.tensor_tensor(out=czz.t[:, lo:hi], in0=ps.t[:, 0:hi - lo], in1=pu.t[:, lo:hi], op=ALU.mult),
                             reads=ps.toks + pu.toks, writes=czz.toks)
                    inproj_chunk(wb, 256 + mm * 128, True, consume_v)
                    wc = cb + C_CCW
                    S.op("dve", lambda e, j=j, wc=wc: e.tensor_scalar(out=cacc.t[:], in0=czz.t[:, H:TH], scalar1=col(wc + 2 * 8 + j), scalar2=None, op0=ALU.mult),
                         reads=czz.toks + cols.toks, writes=cacc.toks)
                    for k in (1, 0):
                        sh = 2 - k
                        S.op("dve", lambda e, k=k, sh=sh, j=j, wc=wc: e.scalar_tensor_tensor(
                            out=cacc.t[:], in0=czz.t[:, H - sh:TH - sh], scalar=col(wc + k * 8 + j), in1=cacc.t[:], op0=ALU.mult, op1=ALU.add),
                            reads=czz.toks + cacc.toks + cols.toks, writes=cacc.toks)
                    S.op("dve", lambda e, j=j: e.tensor_tensor(out=conv_out.t[:, j, :], in0=cacc.t[:], in1=conv_out.t[:, j, :], op=ALU.mult),
                         reads=cacc.toks + [conv_out.toks[j]], writes=[conv_out.toks[j]])

            pc_temps = praw[0].toks + praw[1].toks + pu.toks + dTt.toks
            for mb in range(4):
                for k in range(3):
                    wb = wload([(wl[:, OFF_GL + k * 2048 + mb * 512:OFF_GL + k * 2048 + (mb + 1) * 512], 0)], 16)
                    for mm in range(4):
                        m = mb * 4 + mm
                        gi = k * 4 + mm

                        def consume(part, ps, gi=gi, k=k, m=m):
                            sl = slice(part * 512, (part + 1) * 512)
                            S.op("act", lambda e: e.activation(out=gates.t[:, gi, sl], in_=ps.t[:, 0:512], func=AF.Sigmoid,
                                                               bias=col(cb + C_BG + k * 16 + m), scale=1.0),
                                 reads=ps.toks + cols.toks, writes=[gates.toks[gi]] + pc_temps)
                        inproj_chunk(wb, mm * 128, False, consume)
                wbp = wload([(w_bp[l][:, mb * 512:(mb + 1) * 512], 0)], 8)
                for k, (wsrc, kcs, src_buf) in enumerate(((w_bp, 8, pool_out), (w_bs, 16, ssm_out), (w_bc, 8, conv_out))):
                    if k == 0:
                        wbk = wbp
                    else:
                        wbk = wload([(wsrc[l][:, mb * 512:(mb + 1) * 512], 0)], kcs)
                    for mm in range(4):
                        m = mb * 4 + mm
                        gi = k * 4 + mm
                        for part in range(2):
                            sl = slice(part * 512, (part + 1) * 512)
                            ps = psum()
                            mm_group(ps, wbk, mm * 128, kcs, lambda kc, sl=sl, src_buf=src_buf: src_buf.t[:, kc, sl], src_buf.toks, 512)
                            S.op("dve", lambda e, gi=gi, sl=sl, ps=ps: e.tensor_tensor(out=gates.t[:, gi, sl], in0=ps.t[:, 0:512], in1=gates.t[:, gi, sl], op=ALU.mult),
                                 reads=ps.toks + [gates.toks[gi]], writes=[gates.toks[gi]])
                for mm in range(4):
                    m = mb * 4 + mm
                    for part in range(2):
                        sl = slice(part * 512, (part + 1) * 512)
                        ma = macc[part]
                        S.op("dve", lambda e, mm=mm, sl=sl, ma=ma: e.tensor_tensor(out=ma.t[:], in0=gates.t[:, mm, sl], in1=gates.t[:, 4 + mm, sl], op=ALU.add),
                             reads=[gates.toks[mm], gates.toks[4 + mm]], writes=ma.toks)
                        S.op("dve", lambda e, mm=mm, sl=sl, ma=ma, m=m: e.tensor_tensor(out=merged.t[:, m, sl], in0=ma.t[:], in1=gates.t[:, 8 + mm, sl], op=ALU.add),
                             reads=ma.toks + [gates.toks[8 + mm]], writes=[merged.toks[m]] + u2_users)

            for n in range(16):
                S.dma("sp", lambda e: e.dma_start(out=xT.t[:, n, :], in_=xspill[:, n, :]), sem_xr[n], reads=[],
                      writes=[xT.toks[n]] + (u0_users if n == 0 else []))
            psn_c = [PS[5], PS[6]]
            for nb in range(4):
                wb = wload([(w_o[l][:, nb * 512:(nb + 1) * 512], 0)], 16)
                for mm in range(4):
                    n = nb * 4 + mm
                    for part in range(2):
                        sl = slice(part * 512, (part + 1) * 512)
                        ps = psum()
                        mm_group(ps, wb, mm * 128, 16, lambda kc, sl=sl: merged.t[:, kc, sl], merged.toks, 512)
                        if part == 0:
                            flush_pe_pending()
                        S.op("dve", lambda e: e.tensor_tensor(out=xT.t[:, n, sl], in0=ps.t[:, 0:512], in1=xT.t[:, n, sl], op=ALU.add),
                             reads=ps.toks + [xT.toks[n]], writes=[xT.toks[n]])
                    sq = sqt[n % 2]
                    S.op("act", lambda e: e.activation(out=sq.t[:, H:TH], in_=xT.t[:, n, :], func=AF.Square),
                         reads=[xT.toks[n]], writes=sq.toks)

                    def mk(n=n, sq=sq):
                        def f():
                            for part in range(2):
                                lo, hi = (H, H + 512) if part == 0 else (H + 512, TH)
                                S.op("pe", lambda e: e.matmul(psn_c[part].t[:, 0:512], ONE_b, sq.t[:, lo:hi], start=(n == 0), stop=(n == 15)),
                                     reads=sq.toks + [CSTB], writes=psn_c[part].toks)
                        return f
                    pe_pending.append(mk())
            flush_pe_pending()
            if dbg and l == 0:
                pass
            if l + 1 < NL:
                S.dma("sp", lambda e: e.dma_start(out=pay_h[l].rearrange("p (k h) -> p k h", k=16), in_=xT.t[:, :, T - H:T]), sem_pay, reads=xT.toks)
                pay_wait = (sem_pay, S.count[sem_pay])
                S.ops["pool"].append(([pay_wait], None, None, 0))
                cch = Tok("cch")
                S.dma("pool", lambda e: e.collective_compute(
                    "AllGather", ALU.bypass, replica_groups=[[0, 1, 2, 3], [4, 5, 6, 7]],
                    ins=[pay_h[l]], outs=[gat_h[l]]), next_cc_sem(), writes=[cch], inc=1)
                S.dma("sp", lambda e: e.dma_start(out=ghalo.t[:], in_=gat_h[l].rearrange("(r p) f -> p r f", r=4)), sem_gin,
                      reads=[cch], writes=ghalo.toks)
            mr = [t for mt in merged.toks for t in mt.r]
            for tk in u2_users:
                tk.r.extend(mr)
            sr = [t for mt in ssm_out.toks for t in mt.r]
            for tk in u1_users:
                tk.r.extend(sr)
        if not final_norm:
            S.dma("sp", lambda e: e.dma_start(out=yT_out, in_=xT.t[:]), sem_out, reads=xT.toks)
        S.drain("sp", [sem_out, sem_x, sem_pay, sem_gin])
        S.emit()
    return nc


def _cols_pack(v):
    v = np.asarray(v, np.float32).reshape(-1, 128)
    return v.T


def prep_inputs(inputs):
    f32 = np.float32
    x = np.asarray(inputs["x"], f32)
    cols = np.zeros((128, NCOLS), f32)
    for l in range(DEPTH):
        b = l * NCOL_L
        cols[:, b + C_NW:b + C_NW + 16] = _cols_pack(inputs["norm_w"][l])
        cols[:, b + C_BG:b + C_BG + 48] = _cols_pack(inputs["b_gate"][l])
        cols[:, b + C_PSC:b + C_PSC + 8] = _cols_pack(inputs["pool_scale"][l])
        for k in range(4):
            cols[:, b + C_SCW + k * 24:b + C_SCW + (k + 1) * 24] = _cols_pack(inputs["ssm_conv_w"][l][k])
        cols[:, b + C_SCB:b + C_SCB + 24] = _cols_pack(inputs["ssm_conv_b"][l])
        cols[:, b + C_SNW:b + C_SNW + 16] = _cols_pack(inputs["ssm_norm_w"][l])
        for k in range(3):
            cols[:, b + C_CCW + k * 8:b + C_CCW + (k + 1) * 8] = _cols_pack(inputs["sc_conv_w"][l][k])
    cols[:, DEPTH * NCOL_L:DEPTH * NCOL_L + 16] = _cols_pack(inputs["final_norm_w"])
    rows = np.zeros((DEPTH, 96), f32)
    for l in range(DEPTH):
        rows[l, 0:32] = inputs["ssm_dt_bias"][l]
        rows[l, 32:64] = inputs["ssm_a_log"][l]
        rows[l, 64:96] = inputs["ssm_d"][l]
    rows = np.ascontiguousarray(np.broadcast_to(rows.reshape(1, -1), (128, DEPTH * 96)))
    i = np.arange(128)
    U = (i[:, None] <= i[None, :]).astype(f32)
    L = (i[:, None] > i[None, :]).astype(f32)
    cst = np.concatenate([U, L, np.ones((128, 128), f32), np.eye(128, dtype=f32)], axis=1)
    shared = {
        "w_in": np.asarray(inputs["w_in"], f32), "w_br_pool": np.asarray(inputs["w_br_pool"], f32),
        "w_br_ssm": np.asarray(inputs["w_br_ssm"], f32), "w_br_conv": np.asarray(inputs["w_br_conv"], f32),
        "w_out": np.asarray(inputs["w_out"], f32), "pool_w": np.asarray(inputs["pool_w"], f32),
        "cols": cols, "rows": rows, "cst": cst,
    }
    in_maps = []
    for r in range(8):
        b, q = r // 4, r % 4
        xs = x[b, q * T:(q + 1) * T, :]
        xT = np.ascontiguousarray(xs.T.reshape(16, 128, T).transpose(1, 0, 2))
        if q == 0:
            xh = np.zeros((128, 16, H), f32)
        else:
            hh = x[b, q * T - H:q * T, :]
            xh = np.ascontiguousarray(hh.T.reshape(16, 128, H).transpose(1, 0, 2))
        pc = np.zeros((128, 164), f32)
        if q > 0:
            pc[:, q - 1] = 1.0
        negmask = np.zeros((3, 32), f32)
        for j in range(3):
            if j >= q:
                negmask[j, :] = -30000.0
        pc[:, 4:100] = negmask.reshape(1, 96)
        corr = np.ones((4, 16), f32)
        if q == 0:
            for g, win in enumerate(POOL_WINDOWS):
                for t in range(16):
                    corr[g, t] = win / min(t + 1, win)
        pc[:, 100:164] = corr.reshape(1, 64)
        wsel = np.zeros((4, 3, 128), f32)
        for j in range(3):
            for i_ in range(4):
                if j < i_ < q:
                    wsel[i_, j, :] = 1.0
        m = dict(shared)
        m.update({"xT": xT, "xh": xh, "pcore": pc, "wsel": wsel.reshape(4, 384)})
        in_maps.append(m)
    return in_maps


def assemble(results, key="yT"):
    out = np.zeros((2, 4 * T, D), np.float32)
    for r in range(8):
        b, q = r // 4, r % 4
        yT = np.asarray(results[r][key])
        out[b, q * T:(q + 1) * T, :] = yT.transpose(1, 0, 2).reshape(D, T).T
    return out


_NC_CACHE = {}


def kernel(**inputs):
    in_maps = prep_inputs(inputs)
    if "nc" not in _NC_CACHE:
        _NC_CACHE["nc"] = build_program()
    res = run_bass_kernel_spmd(_NC_CACHE["nc"], in_maps, core_ids=list(range(8)))
    return assemble(res.results)
```
